# Optimizing a Trainium2 kernel written in Bass

```python
import math
import jax, jax.numpy as jnp
from jax import lax
import numpy as np

D_MODEL = 2048
BATCH = 1
SEQ = 16384
DEPTH = 1
DEC_BATCH = 8
DEC_SEQ = 2048
PAST_LEN = 128

MLA_HEADS = 8
MLA_NOPE = 128
MLA_ROPE = 64
MLA_QK = MLA_NOPE + MLA_ROPE
MLA_V = 128
Q_LORA = 512
KV_LORA = 256
ROPE_THETA = 10000.0
MLA_OUT = MLA_HEADS * MLA_V
DIFF_HEADS = 8
DIFF_QK = 64
DIFF_V = 2 * DIFF_QK
DIFF_OUT = DIFF_HEADS * DIFF_V
REL_BUCKETS = 32
REL_MAX_DIST = 128
D_FF = -(-(8 * D_MODEL) // (3 * 256)) * 256
Q_BLOCK = 128
EPS = 1e-6
COL_SIZES = (Q_LORA, KV_LORA, MLA_ROPE,
             DIFF_HEADS * 2 * DIFF_QK, DIFF_HEADS * 2 * DIFF_QK, DIFF_OUT,
             D_MODEL, D_MODEL)
D_IN = sum(COL_SIZES)

kernel_name = "hybrid_mla_diffattn_gated_encoder"


def _split_points():
    pts, acc = [], 0
    for s in COL_SIZES[:-1]:
        acc += s
        pts.append(acc)
    return pts


def rmsnorm(x, g):
    xf = x.astype(jnp.float32)
    y = xf * lax.rsqrt(jnp.mean(xf * xf, axis=-1, keepdims=True) + EPS)
    return (y * g.astype(jnp.float32)).astype(x.dtype)


def rope(x, pos):
    half = x.shape[-1] // 2
    inv = ROPE_THETA ** (-jnp.arange(half, dtype=jnp.float32) / half)
    ang = pos[:, None].astype(jnp.float32) * inv[None, :]
    cos = jnp.cos(ang)[None, :, None, :].astype(x.dtype)
    sin = jnp.sin(ang)[None, :, None, :].astype(x.dtype)
    x1, x2 = x[..., :half], x[..., half:]
    return jnp.concatenate([x1 * cos - x2 * sin, x2 * cos + x1 * sin], axis=-1)


def t5_bucket(rel):
    nb = REL_BUCKETS // 2
    ret = jnp.where(rel > 0, nb, 0)
    n = jnp.abs(rel)
    max_exact = nb // 2
    nf = jnp.maximum(n, 1).astype(jnp.float32)
    large = max_exact + (jnp.log(nf / max_exact) / math.log(REL_MAX_DIST / max_exact)
                         * (nb - max_exact)).astype(jnp.int32)
    large = jnp.minimum(large, nb - 1)
    return ret + jnp.where(n < max_exact, n, large)


def mla_attend(q, k, v):
    B, S, H, D = q.shape
    nb = S // Q_BLOCK
    qb = q.reshape(B, nb, Q_BLOCK, H, D).swapaxes(0, 1)
    scale = D ** -0.5

    def blk(qi):
        s = jnp.einsum('bqhd,bkhd->bhqk', qi, k).astype(jnp.float32) * scale
        p = jax.nn.softmax(s, axis=-1)
        return jnp.einsum('bhqk,bkhd->bqhd', p.astype(v.dtype), v)

    o = lax.map(blk, qb)
    return o.swapaxes(0, 1).reshape(B, S, H * v.shape[-1])


def diff_attend(q, k, v, lam, rel_bias):
    B, S, H, _, D = q.shape
    nb = S // Q_BLOCK
    qb = q.reshape(B, nb, Q_BLOCK, H, 2, D).swapaxes(0, 1)
    kpos = jnp.arange(S, dtype=jnp.int32)
    scale = D ** -0.5

    def blk(args):
        qi, i = args
        qpos = i * Q_BLOCK + jnp.arange(Q_BLOCK, dtype=jnp.int32)
        bias = rel_bias[t5_bucket(kpos[None, :] - qpos[:, None])]
        bias = jnp.transpose(bias, (2, 0, 1)).astype(jnp.float32)[None, :, None]
        s = jnp.einsum('bqhmd,bkhmd->bhmqk', qi, k).astype(jnp.float32) * scale + bias
        p = jax.nn.softmax(s, axis=-1)
        a = p[:, :, 0] - lam * p[:, :, 1]
        return jnp.einsum('bhqk,bkhd->bqhd', a.astype(v.dtype), v)

    o = lax.map(blk, (qb, jnp.arange(nb, dtype=jnp.int32)))
    return o.swapaxes(0, 1).reshape(B, S, H, v.shape[-1])


def encoder_layer(x, l, p, rel_bias):
    B, S, _ = x.shape
    pos = jnp.arange(S, dtype=jnp.int32)
    h = rmsnorm(x, p['mix_norm'][l])
    z = h @ p['w_in'][l]
    c_q, c_kv, k_pe, dq, dk, dv, g_a, g_b = jnp.split(z, _split_points(), axis=-1)

    q = (rmsnorm(c_q, p['q_a_norm'][l]) @ p['wq_b'][l]).reshape(B, S, MLA_HEADS, MLA_QK)
    kv = (rmsnorm(c_kv, p['kv_a_norm'][l]) @ p['wkv_b'][l]).reshape(B, S, MLA_HEADS, MLA_NOPE + MLA_V)
    k_nope, v_a = kv[..., :MLA_NOPE], kv[..., MLA_NOPE:]
    k_pe = jnp.broadcast_to(k_pe[:, :, None, :], (B, S, MLA_HEADS, MLA_ROPE))
    k = jnp.concatenate([k_nope, k_pe], axis=-1)
    q = rmsnorm(q, p['mla_q_norm'][l])
    k = rmsnorm(k, p['mla_k_norm'][l])
    q = jnp.concatenate([q[..., :MLA_NOPE], rope(q[..., MLA_NOPE:], pos)], axis=-1)
    k = jnp.concatenate([k[..., :MLA_NOPE], rope(k[..., MLA_NOPE:], pos)], axis=-1)
    o_a = mla_attend(q, k, v_a)

    lam_init = 0.8 - 0.6 * math.exp(-0.3 * l)
    lam = (jnp.exp(jnp.sum(p['lambda_q1'][l].astype(jnp.float32) * p['lambda_k1'][l].astype(jnp.float32)))
           - jnp.exp(jnp.sum(p['lambda_q2'][l].astype(jnp.float32) * p['lambda_k2'][l].astype(jnp.float32)))
           + lam_init)
    qd = rmsnorm(dq.reshape(B, S, DIFF_HEADS, 2, DIFF_QK), p['diff_q_norm'][l])
    kd = rmsnorm(dk.reshape(B, S, DIFF_HEADS, 2, DIFF_QK), p['diff_k_norm'][l])
    vd = dv.reshape(B, S, DIFF_HEADS, DIFF_V)
    o_b = diff_attend(qd, kd, vd, lam, rel_bias)
    o_b = (rmsnorm(o_b, p['diff_subln'][l]) * (1.0 - lam_init)).reshape(B, S, DIFF_OUT)

    merged = (jax.nn.sigmoid(g_a) * (o_a @ p['w_up_mla'][l])
              + jax.nn.sigmoid(g_b) * (o_b @ p['w_up_diff'][l]))
    x = x + merged @ p['w_o'][l]

    h = rmsnorm(x, p['ffn_norm'][l])
    x = x + (jax.nn.silu(h @ p['w_gate'][l]) * (h @ p['w_up'][l])) @ p['w_down'][l]
    return x


def setup_inputs(seed: int = 0) -> dict:
    key = jax.random.key(seed)
    ks = jax.random.split(key, 32)
    f32 = jnp.float32

    def w(k, shape, fan_in):
        return jax.random.normal(k, shape, f32) * fan_in ** -0.5

    def g(k, shape):
        return 1.0 + 0.02 * jax.random.normal(k, shape, f32)

    L = DEPTH
    return {
        'x_prompt': jax.random.normal(ks[0], (BATCH, SEQ, D_MODEL), f32),
        'x_sample': jax.random.normal(ks[1], (DEC_BATCH, DEC_SEQ, D_MODEL), f32),
        'mix_norm': g(ks[2], (L, D_MODEL)),
        'w_in': w(ks[3], (L, D_MODEL, D_IN), D_MODEL),
        'q_a_norm': g(ks[4], (L, Q_LORA)),
        'wq_b': w(ks[5], (L, Q_LORA, MLA_HEADS * MLA_QK), Q_LORA),
        'kv_a_norm': g(ks[6], (L, KV_LORA)),
        'wkv_b': w(ks[7], (L, KV_LORA, MLA_HEADS * (MLA_NOPE + MLA_V)), KV_LORA),
        'mla_q_norm': g(ks[8], (L, MLA_QK)),
        'mla_k_norm': g(ks[9], (L, MLA_QK)),
        'diff_q_norm': g(ks[10], (L, DIFF_QK)),
        'diff_k_norm': g(ks[11], (L, DIFF_QK)),
        'lambda_q1': 0.1 * jax.random.normal(ks[12], (L, DIFF_QK), f32),
        'lambda_k1': 0.1 * jax.random.normal(ks[13], (L, DIFF_QK), f32),
        'lambda_q2': 0.1 * jax.random.normal(ks[14], (L, DIFF_QK), f32),
        'lambda_k2': 0.1 * jax.random.normal(ks[15], (L, DIFF_QK), f32),
        'diff_subln': g(ks[16], (L, DIFF_V)),
        'w_up_mla': w(ks[17], (L, MLA_OUT, D_MODEL), MLA_OUT),
        'w_up_diff': w(ks[18], (L, DIFF_OUT, D_MODEL), DIFF_OUT),
        'w_o': w(ks[19], (L, D_MODEL, D_MODEL), D_MODEL),
        'ffn_norm': g(ks[20], (L, D_MODEL)),
        'w_gate': w(ks[21], (L, D_MODEL, D_FF), D_MODEL),
        'w_up': w(ks[22], (L, D_MODEL, D_FF), D_MODEL),
        'w_down': w(ks[23], (L, D_FF, D_MODEL), D_FF),
        'rel_bias': 0.5 * jax.random.normal(ks[24], (REL_BUCKETS, DIFF_HEADS), f32),
    }


def reference(x_prompt, x_sample, mix_norm, w_in, q_a_norm, wq_b, kv_a_norm, wkv_b,
              mla_q_norm, mla_k_norm, diff_q_norm, diff_k_norm, lambda_q1, lambda_k1,
              lambda_q2, lambda_k2, diff_subln, w_up_mla, w_up_diff, w_o, ffn_norm,
              w_gate, w_up, w_down, rel_bias):
    p = dict(mix_norm=mix_norm, w_in=w_in, q_a_norm=q_a_norm, wq_b=wq_b,
             kv_a_norm=kv_a_norm, wkv_b=wkv_b, mla_q_norm=mla_q_norm, mla_k_norm=mla_k_norm,
             diff_q_norm=diff_q_norm, diff_k_norm=diff_k_norm, lambda_q1=lambda_q1,
             lambda_k1=lambda_k1, lambda_q2=lambda_q2, lambda_k2=lambda_k2,
             diff_subln=diff_subln, w_up_mla=w_up_mla, w_up_diff=w_up_diff, w_o=w_o,
             ffn_norm=ffn_norm, w_gate=w_gate, w_up=w_up, w_down=w_down)
    y_prompt = x_prompt
    y_sample = x_sample
    for l in range(DEPTH):
        y_prompt = encoder_layer(y_prompt, l, p, rel_bias)
        y_sample = encoder_layer(y_sample, l, p, rel_bias)
    return (y_prompt, y_sample)
```

```python
import bisect
import contextlib
import math
import numpy as np
import concourse.bass as bass
import concourse.mybir as mybir
from concourse.bass_utils import run_bass_kernel_spmd

F32 = mybir.dt.float32
BF16 = mybir.dt.bfloat16
ALU = mybir.AluOpType
AF = mybir.ActivationFunctionType
AX = mybir.AxisListType

D_MODEL = 2048
MLA_HEADS = 8
Q_LORA = 512
KV_LORA = 256
D_FF = 5632
EPS = 1e-6
LAM_INIT = 0.8 - 0.6 * math.exp(0.0)
NCORES = 8
CAST_BARRIER = True
T = 512
NBW = 1152
LW = 1279


class Prog:
    COMPUTE = ("pe", "act", "dve", "pool")

    def __init__(self, nc, nsp=8, npool=4):
        self.nc = nc
        self.q = {e: [] for e in ("pe", "act", "dve", "pool", "sp")}
        self.sems = {}
        self.sem_ctx = []
        for e in self.COMPUTE:
            self.sems[e] = self._sem("c_" + e)
        self.dq = {"sp": [self._sem(f"dsp{i}") for i in range(nsp)],
                   "pool": [self._sem(f"dpl{i}") for i in range(npool)]}
        self.dn = {"sp": 0, "pool": 0}
        self.nops = {e: 0 for e in self.COMPUTE}
        self.inc_idx = {e: [] for e in self.COMPUTE}
        self.ents = {e: {} for e in self.COMPUTE}
        self.seen = {e: {} for e in self.q}
        self.last_w = {}
        self.readers = {}

    def _sem(self, name):
        ctx = self.nc.semaphore(name)
        s = ctx.__enter__()
        self.sem_ctx.append(ctx)
        return s

    def _need(self, eng, tok, waits):
        if tok is None:
            return
        if tok[0] == "c":
            _, e, idx = tok
            if e == eng and e == "pe":
                return
            lst = self.inc_idx[e]
            p = bisect.bisect_left(lst, idx)
            if p < len(lst):
                val = p + 1
            else:
                self.ents[e][idx]["inc"] = True
                lst.append(idx)
                val = len(lst)
            sem = self.sems[e]
        else:
            _, sem, val = tok
        sid = id(sem)
        if self.seen[eng].get(sid, 0) >= val:
            return
        self.seen[eng][sid] = val
        waits.append((sem, val))

    @staticmethod
    def _chan(tok):
        return tok[1] if tok[0] == "c" else id(tok[1])

    def _deps(self, eng, reads, writes, is_dma):
        need = {}

        def add(t):
            if t is None:
                return
            if (not is_dma) and t[0] == "c" and t[1] == eng and eng == "pe":
                return
            c = self._chan(t)
            o = need.get(c)
            if o is None or t[2] > o[2]:
                need[c] = t

        for k in reads:
            add(self.last_w.get(k))
        for k in writes:
            t = self.last_w.get(k)
            if t is not None and (is_dma or not (t[0] == "c" and t[1] == eng)):
                add(t)
            for r in self.readers.get(k, {}).values():
                if is_dma or not (r[0] == "c" and r[1] == eng):
                    add(r)
        waits = []
        for t in need.values():
            self._need(eng, t, waits)
        return waits

    def _commit(self, tok, reads, writes):
        c = self._chan(tok)
        for k in reads:
            d = self.readers.setdefault(k, {})
            o = d.get(c)
            if o is None or tok[2] > o[2]:
                d[c] = tok
        for k in writes:
            self.last_w[k] = tok
            self.readers[k] = {}

    def op(self, eng, fn, reads=(), writes=()):
        waits = self._deps(eng, reads, writes, False)
        self.nops[eng] += 1
        idx = self.nops[eng]
        ent = {"fn": fn, "waits": waits, "inc": False}
        self.ents[eng][idx] = ent
        self.q[eng].append(ent)
        tok = ("c", eng, idx)
        self._commit(tok, reads, writes)
        return tok

    def dma(self, out, in_, reads=(), writes=(), queue="sp"):
        waits = self._deps(queue, reads, writes, True)
        pool = self.dq[queue]
        n = self.dn[queue]
        self.dn[queue] += 1
        sem = pool[n % len(pool)]
        rnd = n // len(pool)
        if rnd > 0:
            self._need(queue, ("d", sem, 16 * rnd), waits)
        tok = ("d", sem, 16 * (rnd + 1))
        ent = {"fn": (lambda e, o=out, i=in_: e.dma_start(out=o, in_=i)),
               "waits": waits, "dma": sem}
        self.q[queue].append(ent)
        self._commit(tok, reads, writes)
        return tok

    def drain(self, eng, queues=("sp", "pool")):
        waits = []
        for qn in queues:
            pool = self.dq[qn]
            n = self.dn[qn]
            for i, sem in enumerate(pool):
                cnt = (n - i + len(pool) - 1) // len(pool) if n > i else 0
                if cnt > 0:
                    self._need(eng, ("d", sem, 16 * cnt), waits)
        if waits:
            self.q[eng].append({"fn": None, "waits": waits})

    def emit(self):
        nc = self.nc
        handles = {"pe": "tensor", "act": "scalar", "dve": "vector", "pool": "gpsimd", "sp": "sync"}
        with nc.Block() as block:
            for ename, attr in handles.items():
                ents = self.q[ename]
                if not ents:
                    continue
                csem = self.sems.get(ename)

                def body(eng, ents=ents, csem=csem):
                    for ent in ents:
                        for (s, v) in ent["waits"]:
                            eng.wait_ge(s, v)
                        if ent["fn"] is None:
                            continue
                        ins = ent["fn"](eng)
                        if "dma" in ent:
                            ins.then_inc(ent["dma"], 16)
                        elif ent["inc"]:
                            ins.then_inc(csem, 1)
                getattr(block, attr)(body)
        for ctx in reversed(self.sem_ctx):
            ctx.__exit__(None, None, None)

def t5_bucket_np(rel):
    nb = 16
    max_exact = 8
    rel = np.asarray(rel, dtype=np.int64)
    ret = np.where(rel > 0, nb, 0)
    n = np.abs(rel)
    nf = np.maximum(n, 1).astype(np.float32)
    ratio = np.log(nf / np.float32(max_exact)) / np.float32(math.log(128 / max_exact))
    large = max_exact + (ratio.astype(np.float32) * np.float32(nb - max_exact)).astype(np.int32)
    large = np.minimum(large, nb - 1)
    return (ret + np.where(n < max_exact, n, large)).astype(np.int64)


def onehot32(b):
    return (np.arange(32)[:, None] == np.asarray(b).reshape(1, -1)).astype(np.float32)


def rope_tables(pos):
    half = 32
    inv = (10000.0 ** (-np.arange(half, dtype=np.float32) / half)).astype(np.float32)
    ang = pos.astype(np.float32)[None, :] * inv[:, None]
    cos = np.cos(ang).astype(np.float32)
    sin = np.sin(ang).astype(np.float32)
    c = np.concatenate([cos, cos], axis=0)
    s = np.concatenate([-sin, sin], axis=0)
    return np.ascontiguousarray(c), np.ascontiguousarray(s)


def swap_halves(a, axis=-1):
    h = a.shape[axis] // 2
    lo = np.take(a, np.arange(0, h), axis=axis)
    hi = np.take(a, np.arange(h, 2 * h), axis=axis)
    return np.concatenate([hi, lo], axis=axis)


def host_prep(inp, S_P, S_S):
    NPC = S_P // NCORES
    f = lambda a: np.ascontiguousarray(np.asarray(a, dtype=np.float32))
    xP = f(inp["x_prompt"])[0]
    xS = f(inp["x_sample"])
    w_in = f(inp["w_in"])[0]
    wq_b = f(inp["wq_b"])[0]
    shared = {
        "w_in": w_in,
        "w_kpeP": np.ascontiguousarray(swap_halves(w_in[:, 768:832])),
        "wq_b": wq_b,
        "wq_ropeP": np.ascontiguousarray(
            swap_halves(wq_b.reshape(512, 8, 192)[:, :, 128:192]).reshape(512, 512)),
        "wkv_b": f(inp["wkv_b"])[0],
        "w_up_mla": f(inp["w_up_mla"])[0],
        "w_up_diff": f(inp["w_up_diff"])[0],
        "w_o": f(inp["w_o"])[0],
        "w_gate": f(inp["w_gate"])[0],
        "w_up": f(inp["w_up"])[0],
        "w_down": f(inp["w_down"])[0],
        "relb": f(inp["rel_bias"]),
        "ident": np.eye(128, dtype=np.float32),
    }
    gc = np.zeros((128, 48), np.float32)
    gc[:, 0:16] = f(inp["mix_norm"])[0].reshape(16, 128).T
    gc[:, 16:32] = f(inp["ffn_norm"])[0].reshape(16, 128).T
    gc[:, 32:36] = f(inp["q_a_norm"])[0].reshape(4, 128).T
    gc[:, 36:38] = f(inp["kv_a_norm"])[0].reshape(2, 128).T
    mq = f(inp["mla_q_norm"])[0]
    mk = f(inp["mla_k_norm"])[0]
    gc[:, 38] = mq[0:128]
    gc[0:64, 39] = mq[128:192]
    gc[0:64, 40] = swap_halves(mq[128:192])
    gc[:, 41] = mk[0:128]
    gc[0:64, 42] = mk[128:192]
    gc[0:64, 43] = swap_halves(mk[128:192])
    gc[:, 44] = np.tile(f(inp["diff_q_norm"])[0], 2)
    gc[:, 45] = np.tile(f(inp["diff_k_norm"])[0], 2)
    gc[:, 46] = f(inp["diff_subln"])[0]
    shared["gcols"] = gc
    lam = np.stack([f(inp["lambda_q1"])[0], f(inp["lambda_k1"])[0],
                    f(inp["lambda_q2"])[0], f(inp["lambda_k2"])[0]], 0).reshape(1, 256)
    shared["lamv"] = np.ascontiguousarray(np.tile(lam, (128, 1)))
    shared["ohw"] = onehot32(t5_bucket_np(639 - np.arange(LW)))

    ngp, nkp = NPC // T, S_P // 128
    ngs, nks = S_S // T, S_S // 128
    maps = []
    for c in range(NCORES):
        m = dict(shared)
        m["xp"] = np.ascontiguousarray(np.roll(xP, -c * NPC, axis=0))
        m["xs"] = np.ascontiguousarray(xS[c])
        posP = (np.arange(S_P) + c * NPC) % S_P
        posS = np.arange(S_S)
        cp, sp = rope_tables(posP)
        cs, ss = rope_tables(posS)
        m["ropeC"] = np.ascontiguousarray(np.concatenate([cp, cs], 1))
        m["ropeS"] = np.ascontiguousarray(np.concatenate([sp, ss], 1))
        cols = []
        for j in range(ngp):
            qlo, qhi = posP[j * T], posP[j * T + T - 1]
            for kb in range(nkp):
                klo = posP[kb * 128]
                rel = (klo - qhi) if klo > qhi else (klo + 127 - qlo)
                cols.append(int(t5_bucket_np(rel)))
        for j in range(ngs):
            for kb in range(nks):
                klo = kb * 128
                rel = (klo - (j * T + T - 1)) if klo > j * T + T - 1 else (klo + 127 - j * T)
                cols.append(int(t5_bucket_np(rel)))
        cols += [15, 31]
        m["farsel"] = onehot32(np.array(cols))
        selL = 1.0 if c > 0 else 0.0
        selR = 1.0 if c < NCORES - 1 else 0.0
        m["sel"] = np.ascontiguousarray(
            np.tile(np.array([[selL, 1 - selL, selR, 1 - selR]], np.float32), (128, 1)))
        maps.append(m)
    return maps

class _Stop(Exception):
    pass


def build_program(S_P, S_S, debug=False, stage=9, max_steps=None):
    nc, P, st, body = _build_body(S_P, S_S, debug, stage, max_steps)
    try:
        body()
    except _Stop:
        pass
    P.drain("sp", queues=("sp", "pool"))
    P.emit()
    st.close()
    return nc


def _build_body(S_P, S_S, debug, stage, max_steps=None):
    NPC = S_P // NCORES
    NKEY = S_P + S_S
    NOWN = NPC + S_S
    CH = min(2048, S_S)
    NKB_CH = CH // 128
    ngp, nkp = NPC // T, S_P // 128
    ngs, nks = S_S // T, S_S // 128
    NF = ngp * nkp + ngs * nks + 2
    assert NF <= 1024 and NPC % T == 0 and S_S % T == 0 and S_P % CH == 0

    nc = bass.Bass("TRN2", target_bir_lowering=False)
    P = Prog(nc)

    def din(name, shape):
        return nc.dram_tensor(name, list(shape), F32, kind="ExternalInput")

    def dscr(name, shape, dt=BF16):
        return nc.dram_tensor(name, list(shape), dt, kind="Internal")

    xp_d = din("xp", [S_P, D_MODEL]).ap()
    xs_d = din("xs", [S_S, D_MODEL]).ap()
    w_in_d = din("w_in", [2048, 8000]).ap()
    w_kpeP_d = din("w_kpeP", [2048, 64]).ap()
    wq_b_d = din("wq_b", [512, 1536]).ap()
    wq_ropeP_d = din("wq_ropeP", [512, 512]).ap()
    wkv_b_d = din("wkv_b", [256, 2048]).ap()
    w_up_mla_d = din("w_up_mla", [1024, 2048]).ap()
    w_up_diff_d = din("w_up_diff", [1024, 2048]).ap()
    w_o_d = din("w_o", [2048, 2048]).ap()
    w_gate_d = din("w_gate", [2048, D_FF]).ap()
    w_up_d = din("w_up", [2048, D_FF]).ap()
    w_down_d = din("w_down", [D_FF, 2048]).ap()
    relb_d = din("relb", [32, 8]).ap()
    ident_d = din("ident", [128, 128]).ap()
    gcols_d = din("gcols", [128, 48]).ap()
    lamv_d = din("lamv", [128, 256]).ap()
    ohw_d = din("ohw", [32, LW]).ap()
    ropeC_d = din("ropeC", [64, NKEY]).ap()
    ropeS_d = din("ropeS", [64, NKEY]).ap()
    farsel_d = din("farsel", [32, NF]).ap()
    sel_d = din("sel", [128, 4]).ap()
    yp_d = nc.dram_tensor("yp", [NPC, D_MODEL], F32, kind="ExternalOutput").ap()
    ys_d = nc.dram_tensor("ys", [S_S, D_MODEL], F32, kind="ExternalOutput").ap()

    slabs = {}

    def mkslab(name, n, kc, w):
        slabs[name] = dscr("s_" + name, [n, 128, kc, w]).ap()

    mkslab("cq", 1, 16, 512); mkslab("ckv", 1, 16, 256); mkslab("kpe", 1, 16, 128)
    mkslab("dq", 2, 16, 512); mkslab("dk", 2, 16, 512); mkslab("dv", 2, 16, 512)
    mkslab("ga", 4, 16, 512); mkslab("gb", 4, 16, 512)
    mkslab("wqb", 1, 4, 2048); mkslab("wkvk", 1, 2, 1024); mkslab("wkvv", 1, 2, 1024)
    mkslab("upa", 4, 8, 512); mkslab("upd", 4, 8, 512); mkslab("wo", 4, 16, 512)
    mkslab("wg", 11, 16, 512); mkslab("wu", 11, 16, 512); mkslab("wd", 16, 11, 512)

    kind_dbg = "ExternalOutput" if debug else "Internal"

    def dscr2(name, shape, dt=BF16):
        return nc.dram_tensor(name, list(shape), dt, kind=kind_dbg)

    KTn_d = dscr2("KTn", [8, 128, NKEY]).ap()
    KTp_d = dscr2("KTp", [8, 64, NKEY]).ap()
    KDT_d = dscr2("KDT", [8, 128, NKEY]).ap()
    VA_d = dscr2("VA", [8, NKEY // CH, 128, NKB_CH, 128]).ap()
    VD_d = dscr2("VD", [8, NKEY // CH, 128, NKB_CH, 128]).ap()
    QTn_d = dscr2("QTn", [8, 128, NOWN]).ap()
    QTp_d = dscr2("QTp", [8, 64, NOWN]).ap()
    QDT_d = dscr2("QDT", [8, 128, NOWN]).ap()
    rrep_t = dscr("rrep", [8, 128, LW], F32)
    rrep_d = rrep_t.ap()
    biasD_d = dscr2("biasD", [8, 128, NBW + 1024], F32).ap()
    farbD_d = dscr2("farbD", [8, 128, NF], F32).ap()

    st = contextlib.ExitStack()

    def sb(name, shape, dt):
        return st.enter_context(nc.sbuf_tensor(name, list(shape), dt))

    def pst(name, shape, dt):
        return st.enter_context(nc.psum_tensor(name, list(shape), dt))

    xbuf = sb("xbuf", [128, 4 * 2048], F32)
    hbuf = sb("hbuf", [128, 16 * T], BF16)
    big = sb("big", [128, 48 * T], BF16)
    wslot = [sb(f"wslot{i}", [128, 8192], BF16) for i in range(3)]
    ftmp = sb("ftmp", [128, 8, T], F32)
    btmp = sb("btmp", [128, 6, T], BF16)
    qbuf = sb("qbuf", [128, 2, 1024], BF16)
    biasb = sb("biasb", [128, NBW + 1024], F32)
    farb = sb("farb", [128, NF], F32)
    gc = sb("gc", [128, 48], F32)
    identb = sb("identb", [128, 128], BF16)
    onesb = sb("onesb", [128, 128], BF16)
    oblk = sb("oblk", [128, 128], BF16)
    ones32 = sb("ones32", [32, 128], F32)
    relbT = sb("relbT", [32, 8], F32)
    selt = sb("selt", [128, 4], F32)
    lamt = sb("lamt", [128, 256], F32)
    small = sb("small", [128, 32], F32)
    ropec = sb("ropec", [64, T], F32)
    ropes = sb("ropes", [64, T], F32)
    kperot = sb("kperot", [64, T], F32)
    stat = sb("stat", [128, 16], F32)
    sqkpe = sb("sqkpe", [64, T], BF16)
    psb = [pst(f"ps{i}", [128, T], F32) for i in range(8)]
    psa = [p[:] for p in psb]
    ptr = psa[7].bitcast(BF16)
    assert tuple(ptr.shape) == (128, 2 * T), ptr.shape

    state = {"ps": 0, "psl": list(range(7)), "ft": 0, "bt": 0, "pt": 0}

    def ps_new():
        l = state["psl"]
        i = l[state["ps"] % len(l)]
        state["ps"] += 1
        return psa[i], ("ps", i)

    def ft_new():
        i = state["ft"] % 8
        state["ft"] += 1
        return ftmp[:, i, :], ("ft", i)

    def bt_new():
        i = state["bt"] % 6
        state["bt"] += 1
        return btmp[:, i, :], ("bt", i)

    def pt_new():
        i = state["pt"] % 2
        state["pt"] += 1
        return ptr[:, i * T:(i + 1) * T], ("ps", 7)

    def mm(out, lhsT, rhs, start, stop, reads, writes):
        P.op("pe", lambda e: e.matmul(out, lhsT=lhsT, rhs=rhs, start=start, stop=stop),
             reads=reads, writes=writes)

    def act(out, in_, func, reads, writes, **kw):
        P.op("act", lambda e: e.activation(out=out, in_=in_, func=func, **kw),
             reads=reads, writes=writes)

    def tsc(eng, out, in0, s1, s2, op0, op1, reads, writes):
        if s2 is None:
            P.op(eng, lambda e: e.tensor_scalar(out=out, in0=in0, scalar1=s1, scalar2=None, op0=op0),
                 reads=reads, writes=writes)
        else:
            P.op(eng, lambda e: e.tensor_scalar(out=out, in0=in0, scalar1=s1, scalar2=s2,
                                                op0=op0, op1=op1), reads=reads, writes=writes)

    def tt(eng, out, in0, in1, op, reads, writes):
        P.op(eng, lambda e: e.tensor_tensor(out=out, in0=in0, in1=in1, op=op),
             reads=reads, writes=writes)

    def stt(out, in0, s, in1, op0, op1, reads, writes):
        P.op("dve", lambda e: e.scalar_tensor_tensor(out=out, in0=in0, scalar=s, in1=in1,
                                                      op0=op0, op1=op1), reads=reads, writes=writes)

    def rstd_from(ps_ap, ps_key, dim, npart=128):
        t1, k1 = ft_new()
        act(t1[:npart], ps_ap[:npart], AF.Sqrt, [ps_key], [k1], scale=1.0 / dim, bias=EPS)
        t2, k2 = ft_new()
        P.op("dve", lambda e: e.reciprocal(out=t2[:npart], in_=t1[:npart]), reads=[k1], writes=[k2])
        return t2, k2

    GC_MIX, GC_FFN, GC_QA, GC_KVA = 0, 16, 32, 36
    GC_MQN, GC_MQR, GC_MQRP, GC_MKN, GC_MKR, GC_MKRP, GC_DQ, GC_DK, GC_SUB = 38, 39, 40, 41, 42, 43, 44, 45, 47

    def body():
        P.dma(gc[:], gcols_d, writes=["gc"])
        P.dma(ftmp[:, 0, 0:128], ident_d, writes=[("ft", 0)])
        P.dma(lamt[:], lamv_d, writes=["lamt"])
        P.dma(relbT[:], relb_d, writes=["relbT"])
        P.dma(selt[:], sel_d, writes=["selt"])
        P.op("dve", lambda e: e.tensor_copy(out=identb[:], in_=ftmp[:, 0, 0:128]), reads=[("ft", 0)], writes=["identb"])
        P.op("dve", lambda e: e.memset(onesb[:], 1.0), writes=["onesb"])
        P.op("dve", lambda e: e.memset(oblk[:], 0.0), writes=["oblk"])
        P.op("dve", lambda e: e.memset(oblk[0:64, 0:64], 1.0), writes=["oblk"])
        P.op("dve", lambda e: e.memset(oblk[64:128, 64:128], 1.0), writes=["oblk"])
        P.op("dve", lambda e: e.memset(ones32[:], 1.0), writes=["ones32"])
        tt("dve", ftmp[:, 1, 0:64], lamt[:, 0:64], lamt[:, 64:128], ALU.mult, ["lamt"], [("ft", 1)])
        tt("dve", ftmp[:, 1, 64:128], lamt[:, 128:192], lamt[:, 192:256], ALU.mult, ["lamt"], [("ft", 1)])
        P.op("dve", lambda e: e.reduce_sum(out=small[:, 0:1], in_=ftmp[:, 1, 0:64], axis=AX.X), reads=[("ft", 1)], writes=["sm0"])
        P.op("dve", lambda e: e.reduce_sum(out=small[:, 1:2], in_=ftmp[:, 1, 64:128], axis=AX.X), reads=[("ft", 1)], writes=["sm1"])
        act(small[:, 3:4], small[:, 0:1], AF.Exp, ["sm0"], ["sm3"])
        act(small[:, 4:5], small[:, 1:2], AF.Exp, ["sm1"], ["sm4"])
        tt("dve", small[:, 5:6], small[:, 4:5], small[:, 3:4], ALU.subtract, ["sm3", "sm4"], ["sm5"])
        tsc("dve", small[:, 2:3], small[:, 5:6], -LAM_INIT, None, ALU.add, None, ["sm5"], ["neglam"])
        tsc("dve", gc[:, 47:48], gc[:, 46:47], 1.0 - LAM_INIT, None, ALU.mult, None, ["gc"], ["gc"])

        if stage == 0:
            raise _Stop
        def cast(dst, src, key):
            P.dma(dst, src, writes=[key], queue="pool")

        def cast_cols(name, src, c0, nslab, w):
            for s in range(nslab):
                cast(slabs[name][s], src[:, c0 + s * w:c0 + (s + 1) * w].rearrange("(kc p) n -> p kc n", p=128),
                     ("slab", name, s))

        cast_cols("ckv", w_in_d, 512, 1, 256)
        cast(slabs["kpe"][0][:, :, 0:64], w_in_d[:, 768:832].rearrange("(kc p) n -> p kc n", p=128), ("slab", "kpe", 0))
        cast(slabs["kpe"][0][:, :, 64:128], w_kpeP_d.rearrange("(kc p) n -> p kc n", p=128), ("slab", "kpe", 0))
        for kc in range(2):
            src = wkv_b_d[kc * 128:(kc + 1) * 128, :].rearrange("p (h e) -> p h e", e=256)
            cast(slabs["wkvk"][0][:, kc, :].rearrange("p (h d) -> p h d", d=128), src[:, :, 0:128], ("slab", "wkvk", 0))
            cast(slabs["wkvv"][0][:, kc, :].rearrange("p (h d) -> p h d", d=128), src[:, :, 128:256], ("slab", "wkvv", 0))
        cast_cols("dk", w_in_d, 1856, 2, 512)
        cast_cols("dv", w_in_d, 2880, 2, 512)
        cast_cols("cq", w_in_d, 0, 1, 512)
        for kc in range(4):
            dst = slabs["wqb"][0][:, kc, :].rearrange("p (h d) -> p h d", d=256)
            cast(dst[:, :, 0:192], wq_b_d[kc * 128:(kc + 1) * 128, :].rearrange("p (h d) -> p h d", d=192), ("slab", "wqb", 0))
            cast(dst[:, :, 192:256], wq_ropeP_d[kc * 128:(kc + 1) * 128, :].rearrange("p (h d) -> p h d", d=64), ("slab", "wqb", 0))
        cast_cols("dq", w_in_d, 832, 2, 512)
        cast_cols("ga", w_in_d, 3904, 4, 512)
        cast_cols("gb", w_in_d, 5952, 4, 512)
        cast_cols("upa", w_up_mla_d, 0, 4, 512)
        cast_cols("upd", w_up_diff_d, 0, 4, 512)
        cast_cols("wo", w_o_d, 0, 4, 512)
        cast_cols("wg", w_gate_d, 0, 11, 512)
        cast_cols("wu", w_up_d, 0, 11, 512)
        for n in range(4):
            for part in range(4):
                cast(slabs["wd"][n * 4 + part],
                     w_down_d[part * 1408:(part + 1) * 1408, n * 512:(n + 1) * 512].rearrange("(kc p) n -> p kc n", p=128),
                     ("slab", "wd", n * 4 + part))

        if CAST_BARRIER:
            P.drain("sp", queues=("pool",))
        if stage == 1:
            raise _Stop
        ohw_sb = xbuf[0:32, 0:LW]
        fsel_sb = xbuf[0:32, 2048:2048 + NF]
        P.dma(ohw_sb, ohw_d, writes=["xb0"])
        P.dma(fsel_sb, farsel_d, writes=["xb1"])
        for h in range(8):
            lh = xbuf[0:32, 4096:4096 + 128]
            tsc("dve", lh, ones32[:], relbT[:, h:h + 1], None, ALU.mult, None, ["ones32", "relbT"], ["xb2"])
            wrep = xbuf[:, 6144:6144 + LW]
            for c0 in range(0, LW, 512):
                n = min(512, LW - c0)
                pp, pk = ps_new()
                mm(pp[:, 0:n], lh, ohw_sb[:, c0:c0 + n], True, True, ["xb2", "xb0"], [pk])
                act(wrep[:, c0:c0 + n], pp[:, 0:n], AF.Copy, [pk], ["xb3"])
            P.dma(rrep_d[h], wrep, reads=["xb3"])
            fr = xbuf[0:32, 5120:5120 + NF]
            tsc("dve", fr, fsel_sb, relbT[:, h:h + 1], None, ALU.mult, None, ["xb1", "relbT"], ["xb2b"])
            for c0 in range(0, NF, 512):
                n = min(512, NF - c0)
                pp, pk = ps_new()
                mm(pp[:, 0:n], ones32[:], fr[:, c0:c0 + n], True, True, ["xb2b", "ones32"], [pk])
                act(farb[:, c0:c0 + n], pp[:, 0:n], AF.Copy, [pk], ["farb"])
            P.dma(farbD_d[h], farb[:], reads=["farb"])
            P.op("dve", lambda e, h=h: e.tensor_copy(out=small[:, 8 + h:9 + h], in_=farb[:, NF - 2:NF - 1]), reads=["farb"], writes=[("f15", h)])
            P.op("dve", lambda e, h=h: e.tensor_copy(out=small[:, 16 + h:17 + h], in_=farb[:, NF - 1:NF]), reads=["farb"], writes=[("f31", h)])
        P.drain("sp", queues=("sp",))
        for h in range(8):
            P.dma(biasb[:, 0:NBW], bass.AP(rrep_t, h * 128 * LW + 127, [[LW - 1, 128], [1, NBW]]), writes=["biasb"])
            tt("dve", small[:, 24:25], selt[:, 1:2], small[:, 16 + h:17 + h], ALU.mult, ["selt", ("f31", h)], ["cL"])
            tt("dve", small[:, 25:26], selt[:, 3:4], small[:, 8 + h:9 + h], ALU.mult, ["selt", ("f15", h)], ["cR"])
            tsc("dve", biasb[:, NBW:NBW + 512], biasb[:, 640:1152], selt[:, 0:1], small[:, 24:25], ALU.mult, ALU.add,
                ["biasb", "selt", "cL"], ["biasb"])
            tsc("dve", biasb[:, NBW + 512:NBW + 1024], biasb[:, 0:512], selt[:, 2:3], small[:, 25:26], ALU.mult, ALU.add,
                ["biasb", "selt", "cR"], ["biasb"])
            P.dma(biasD_d[h], biasb[:], reads=["biasb"])
        P.drain("sp", queues=("sp",))

        if stage == 2:
            raise _Stop
        steps = []

        def add_step(fn, loads=None, att=False):
            steps.append((fn, loads, att))

        def slab_load(name, s, n):
            src = slabs[name][s].rearrange("p k w -> p (k w)")
            return [(lambda slot: slot[:, 0:n], src, [("slab", name, s)])]

        def hb(c):
            return hbuf[:, c * T:(c + 1) * T]

        def bg(c):
            return big[:, c * T:(c + 1) * T]

        def xb(t):
            return xbuf[:, t * 2048:(t + 1) * 2048]

        def xsb(t):
            return big[:, (32 + 4 * t) * T:(36 + 4 * t) * T]

        XSB_KEYS = [[("big", 32 + 4 * t + i) for i in range(4)] for t in range(4)]
        HB_KEYS = [("hb", c) for c in range(16)]

        def x_src(k0):
            return xp_d[k0:k0 + T, :] if k0 < S_P else xs_d[k0 - S_P:k0 - S_P + T, :]

        def load_x(k0):
            src = x_src(k0)
            for t in range(4):
                P.dma(xb(t), src[t * 128:(t + 1) * 128, :], writes=[("x", t)])

        def norm_T(goff):
            for t in range(4):
                act(xsb(t), xb(t), AF.Square, [("x", t)], XSB_KEYS[t] + [("st", t)], accum_out=stat[:, t:t + 1])
                act(stat[:, 4 + t:5 + t], stat[:, t:t + 1], AF.Sqrt, [("st", t)], [("st", 4 + t)], scale=1.0 / D_MODEL, bias=EPS)
                P.op("dve", lambda e, t=t: e.reciprocal(out=stat[:, 8 + t:9 + t], in_=stat[:, 4 + t:5 + t]),
                     reads=[("st", 4 + t)], writes=[("st", 8 + t)])
                tsc("dve", xsb(t), xb(t), stat[:, 8 + t:9 + t], None, ALU.mult, None,
                    [("x", t), ("st", 8 + t)], XSB_KEYS[t])

        def transposes(goff):
            for c in range(16):
                pt, pk = pt_new()
                for t in range(4):
                    P.op("pe", lambda e, t=t, c=c, pt=pt: e.transpose(out=pt[:, t * 128:(t + 1) * 128],
                                                                      in_=xsb(t)[:, c * 128:(c + 1) * 128],
                                                                      identity=identb[:]),
                         reads=XSB_KEYS[t] + ["identb"], writes=[pk])
                if c % 2 == 0:
                    tsc("dve", hb(c), pt, gc[:, goff + c:goff + c + 1], None, ALU.mult, None, [pk, "gc"], [HB_KEYS[c]])
                else:
                    P.op("act", lambda e, c=c, pt=pt: e.mul(out=hb(c), in_=pt, mul=gc[:, goff + c:goff + c + 1]),
                         reads=[pk, "gc"], writes=[HB_KEYS[c]])

        def gemm_fm(pp, pk, slot, sk, kcn, w, c0, m, rhs_fn, rhs_keys):
            for kc in range(kcn):
                mm(pp[0:m, :], slot[:, kc * w + c0:kc * w + c0 + m], rhs_fn(kc), kc == 0, kc == kcn - 1,
                   [sk] + rhs_keys, [pk])

        def square_bf(pp, pk, npart=128):
            b, bk = bt_new()
            act(b[0:npart], pp[0:npart], AF.Square, [pk], [bk])
            return b, bk

        groupsA = list(range(NKEY // T))

        def own_off(g):
            k0 = g * T
            if g < ngp:
                return k0
            if k0 >= S_P:
                return NPC + (k0 - S_P)
            return None

        def v_dst(Vd, h0, kt):
            ch, kbi = kt // CH, (kt % CH) // 128
            return Vd[h0:h0 + 4, ch, :, kbi, :].rearrange("h p d -> p h d")

        def phaseA(gi):
            g = groupsA[gi]
            k0 = g * T
            o0 = own_off(g)

            def a_norm(slot, sk):
                if gi == 0:
                    load_x(k0)
                    norm_T(GC_MIX)
                    if 1 < len(groupsA):
                        load_x(groupsA[1] * T)
                transposes(GC_MIX)
            add_step(a_norm)

            def a_prenorm(slot, sk):
                if gi + 1 < len(groupsA):
                    norm_T(GC_MIX)
                    if gi + 2 < len(groupsA):
                        load_x(groupsA[gi + 2] * T)

            def a_ckv(slot, sk):
                pps = []
                sqs = []
                for cc in range(2):
                    pp, pk = ps_new()
                    gemm_fm(pp, pk, slot, sk, 16, 256, cc * 128, 128, hb, HB_KEYS)
                    pps.append((pp, pk))
                    sqs.append(square_bf(pp, pk))
                p2, p2k = ps_new()
                for cc in range(2):
                    mm(p2, onesb[:], sqs[cc][0], cc == 0, cc == 1, ["onesb", sqs[cc][1]], [p2k])
                rs, rk = rstd_from(p2, p2k, KV_LORA)
                for cc in range(2):
                    stt(bg(cc), pps[cc][0], gc[:, GC_KVA + cc:GC_KVA + cc + 1], rs, ALU.mult, ALU.mult,
                        [pps[cc][1], "gc", rk], [("big", cc)])
            add_step(a_ckv, slab_load("ckv", 0, 16 * 256))

            def a_kpe(slot, sk):
                P.dma(ropec[:], ropeC_d[:, k0:k0 + T], writes=["ropec"])
                P.dma(ropes[:], ropeS_d[:, k0:k0 + T], writes=["ropes"])
                pr, prk = ps_new()
                gemm_fm(pr, prk, slot, sk, 16, 128, 0, 64, hb, HB_KEYS)
                prp, prpk = ps_new()
                gemm_fm(prp, prpk, slot, sk, 16, 128, 64, 64, hb, HB_KEYS)
                act(sqkpe[:], pr[0:64], AF.Square, [prk], ["sqkpe"])
                t1, k1 = ft_new()
                stt(t1[0:64], pr[0:64], gc[0:64, GC_MKR:GC_MKR + 1], ropec[:], ALU.mult, ALU.mult, [prk, "gc", "ropec"], [k1])
                t2, k2 = ft_new()
                stt(t2[0:64], prp[0:64], gc[0:64, GC_MKRP:GC_MKRP + 1], ropes[:], ALU.mult, ALU.mult, [prpk, "gc", "ropes"], [k2])
                tt("dve", kperot[:], t1[0:64], t2[0:64], ALU.add, [k1, k2], ["kperot"])
            add_step(a_kpe, slab_load("kpe", 0, 16 * 128))

            def a_wkvk(slot, sk):
                for h in range(8):
                    pn, pnk = ps_new()
                    gemm_fm(pn, pnk, slot, sk, 2, 1024, h * 128, 128, bg, [("big", 0), ("big", 1)])
                    sq, sqk = square_bf(pn, pnk)
                    p2, p2k = ps_new()
                    mm(p2, onesb[:], sq, True, False, ["onesb", sqk], [p2k])
                    mm(p2, onesb[0:64, :], sqkpe[:], False, True, ["onesb", "sqkpe"], [p2k])
                    rs, rk = rstd_from(p2, p2k, 192)
                    kn, knk = bt_new()
                    stt(kn, pn, gc[:, GC_MKN:GC_MKN + 1], rs, ALU.mult, ALU.mult, [pnk, "gc", rk], [knk])
                    P.dma(KTn_d[h, :, k0:k0 + T], kn, reads=[knk])
                    kp, kpk = bt_new()
                    tt("dve", kp[0:64], kperot[:], rs[0:64], ALU.mult, ["kperot", rk], [kpk])
                    P.dma(KTp_d[h, :, k0:k0 + T], kp[0:64], reads=[kpk])
            add_step(a_wkvk, slab_load("wkvk", 0, 2 * 1024))

            def a_wkvv(slot, sk):
                for t in range(4):
                    for n in range(2):
                        pp, pk = ps_new()
                        for cc in range(2):
                            mm(pp, bg(cc)[:, t * 128:(t + 1) * 128], slot[:, cc * 1024 + n * 512:cc * 1024 + (n + 1) * 512],
                               cc == 0, cc == 1, [sk, ("big", cc)], [pk])
                        b, bk = bt_new()
                        act(b, pp, AF.Copy, [pk], [bk])
                        P.dma(v_dst(VA_d, n * 4, k0 + t * 128), b.rearrange("p (h d) -> p h d", d=128), reads=[bk])
            add_step(a_wkvv, slab_load("wkvv", 0, 2 * 1024))

            def mk_dkq(name, s, gcol, dst_d, off):
                def fn(slot, sk):
                    for i in range(4):
                        h = s * 4 + i
                        pp, pk = ps_new()
                        gemm_fm(pp, pk, slot, sk, 16, 512, i * 128, 128, hb, HB_KEYS)
                        sq, sqk = square_bf(pp, pk)
                        p2, p2k = ps_new()
                        mm(p2, oblk[:], sq, True, True, ["oblk", sqk], [p2k])
                        rs, rk = rstd_from(p2, p2k, 64)
                        kd, kdk = bt_new()
                        stt(kd, pp, gc[:, gcol:gcol + 1], rs, ALU.mult, ALU.mult, [pk, "gc", rk], [kdk])
                        P.dma(dst_d[h, :, off:off + T], kd, reads=[kdk])
                return fn
            add_step(a_prenorm)
            for s in range(2):
                add_step(mk_dkq("dk", s, GC_DK, KDT_d, k0), slab_load("dk", s, 16 * 512))

            def mk_dv(s):
                def fn(slot, sk):
                    for t in range(4):
                        pp, pk = ps_new()
                        for kc in range(16):
                            mm(pp, hb(kc)[:, t * 128:(t + 1) * 128], slot[:, kc * 512:(kc + 1) * 512], kc == 0, kc == 15,
                               [sk, HB_KEYS[kc]], [pk])
                        b, bk = bt_new()
                        act(b, pp, AF.Copy, [pk], [bk])
                        P.dma(v_dst(VD_d, s * 4, k0 + t * 128), b.rearrange("p (h d) -> p h d", d=128), reads=[bk])
                return fn
            for s in range(2):
                add_step(mk_dv(s), slab_load("dv", s, 16 * 512))

            if o0 is None:
                return

            def a_cq(slot, sk):
                pps, sqs = [], []
                for cc in range(4):
                    pp, pk = ps_new()
                    gemm_fm(pp, pk, slot, sk, 16, 512, cc * 128, 128, hb, HB_KEYS)
                    pps.append((pp, pk))
                    sqs.append(square_bf(pp, pk))
                p2, p2k = ps_new()
                for cc in range(4):
                    mm(p2, onesb[:], sqs[cc][0], cc == 0, cc == 3, ["onesb", sqs[cc][1]], [p2k])
                rs, rk = rstd_from(p2, p2k, Q_LORA)
                for cc in range(4):
                    stt(bg(2 + cc), pps[cc][0], gc[:, GC_QA + cc:GC_QA + cc + 1], rs, ALU.mult, ALU.mult,
                        [pps[cc][1], "gc", rk], [("big", 2 + cc)])
            add_step(a_cq, slab_load("cq", 0, 16 * 512))

            CQK = [("big", 2 + cc) for cc in range(4)]

            def a_wqb(slot, sk):
                cqn = lambda cc: bg(2 + cc)
                for h in range(8):
                    pn, pnk = ps_new()
                    gemm_fm(pn, pnk, slot, sk, 4, 2048, h * 256, 128, cqn, CQK)
                    pr, prk = ps_new()
                    gemm_fm(pr, prk, slot, sk, 4, 2048, h * 256 + 128, 64, cqn, CQK)
                    prp, prpk = ps_new()
                    gemm_fm(prp, prpk, slot, sk, 4, 2048, h * 256 + 192, 64, cqn, CQK)
                    sqn, sqnk = square_bf(pn, pnk)
                    sqr, sqrk = square_bf(pr, prk, 64)
                    p2, p2k = ps_new()
                    mm(p2, onesb[:], sqn, True, False, ["onesb", sqnk], [p2k])
                    mm(p2, onesb[0:64, :], sqr[0:64], False, True, ["onesb", sqrk], [p2k])
                    rs, rk = rstd_from(p2, p2k, 192)
                    qn, qnk = bt_new()
                    stt(qn, pn, gc[:, GC_MQN:GC_MQN + 1], rs, ALU.mult, ALU.mult, [pnk, "gc", rk], [qnk])
                    P.dma(QTn_d[h, :, o0:o0 + T], qn, reads=[qnk])
                    t1, k1 = ft_new()
                    stt(t1[0:64], pr[0:64], gc[0:64, GC_MQR:GC_MQR + 1], ropec[:], ALU.mult, ALU.mult, [prk, "gc", "ropec"], [k1])
                    t2, k2 = ft_new()
                    stt(t2[0:64], prp[0:64], gc[0:64, GC_MQRP:GC_MQRP + 1], ropes[:], ALU.mult, ALU.mult, [prpk, "gc", "ropes"], [k2])
                    t3, k3 = ft_new()
                    tt("dve", t3[0:64], t1[0:64], t2[0:64], ALU.add, [k1, k2], [k3])
                    qp, qpk = bt_new()
                    tt("dve", qp[0:64], t3[0:64], rs[0:64], ALU.mult, [k3, rk], [qpk])
                    P.dma(QTp_d[h, :, o0:o0 + T], qp[0:64], reads=[qpk])
            add_step(a_wqb, slab_load("wqb", 0, 4 * 2048))

            for s in range(2):
                add_step(mk_dkq("dq", s, GC_DQ, QDT_d, o0), slab_load("dq", s, 16 * 512))

        for gi in range(len(groupsA)):
            phaseA(gi)
        add_step(lambda slot, sk: P.drain("sp", queues=("sp",)))

        SC_MLA = 192.0 ** -0.5
        SC_DIF = 64.0 ** -0.5

        class Pipe:
            def __init__(self):
                self.q1 = None
                self.q2 = None

            def push(self, item):
                item[0]()
                if self.q1 is not None:
                    self.q1[1]()
                if self.q2 is not None:
                    self.q2[2]()
                self.q2 = self.q1
                self.q1 = item

            def flush(self):
                if self.q1 is not None:
                    self.q1[1]()
                if self.q2 is not None:
                    self.q2[2]()
                if self.q1 is not None:
                    self.q1[2]()
                self.q1 = self.q2 = None

        def own_group(seq, j):
            if seq == "p":
                o0, kt_base, nkb, fcol0 = j * T, 0, nkp, j * nkp
                ysrc = yp_d[j * T:(j + 1) * T, :]
                xk0 = j * T
                nown_kb, ng = NPC // 128, ngp
            else:
                o0, kt_base, nkb, fcol0 = NPC + j * T, S_P, nks, ngp * nkp + j * nks
                ysrc = ys_d[j * T:(j + 1) * T, :]
                xk0 = S_P + j * T
                nown_kb, ng = nks, ngs
            nch = nkb * 128 // CH

            def bias_kind(kb):
                if kb < nown_kb:
                    D = kb - 4 * j
                    if -1 <= D <= 4:
                        return ("tile", 512 - 128 * D)
                elif seq == "p":
                    if j == 0 and kb == nkb - 1:
                        return ("tile", NBW)
                    if j == ng - 1 and kb == nown_kb:
                        return ("tile", NBW + 512)
                return ("far", fcol0 + kb)

            for h in range(8):
                pipe = Pipe()
                for ci in range(nch):
                    kt0 = kt_base + ci * CH
                    chg = kt0 // CH
                    loads = [
                        (lambda slot: slot[:, 0:CH], KTn_d[h, :, kt0:kt0 + CH], []),
                        (lambda slot: slot[0:64, CH:2 * CH], KTp_d[h, :, kt0:kt0 + CH], []),
                        (lambda slot: slot[:, 2 * CH:3 * CH], VA_d[h, chg].rearrange("p k d -> p (k d)"), []),
                    ]

                    def fn(slot, sk, h=h, ci=ci, pipe=pipe):
                        b = h % 2
                        if h == 0 and ci == 0:
                            state["psl"] = [0, 1, 2, 5, 6, 7]
                            load_x(xk0)
                            P.dma(qbuf[:, 0, 0:T], QTn_d[0, :, o0:o0 + T], writes=[("q", 0)])
                            P.dma(qbuf[0:64, 0, T:2 * T], QTp_d[0, :, o0:o0 + T], writes=[("q", 0)])
                        if ci == 0 and h + 1 < 8:
                            P.dma(qbuf[:, 1 - b, 0:T], QTn_d[h + 1, :, o0:o0 + T], writes=[("q", 1 - b)])
                            P.dma(qbuf[0:64, 1 - b, T:2 * T], QTp_d[h + 1, :, o0:o0 + T], writes=[("q", 1 - b)])
                        qn = qbuf[:, b, 0:T]
                        qp = qbuf[0:64, b, T:2 * T]
                        for kbi in range(NKB_CH):
                            kb = ci * NKB_CH + kbi
                            first, last = (kb == 0), (kb == nkb - 1)
                            cell = {}

                            def s_fn(cell=cell, kbi=kbi):
                                pS, pSk = ps_new()
                                cell["pS"] = (pS, pSk)
                                mm(pS, slot[:, kbi * 128:(kbi + 1) * 128], qn, True, False, [sk, ("q", b)], [pSk])
                                mm(pS, slot[0:64, CH + kbi * 128:CH + (kbi + 1) * 128], qp, False, True, [sk, ("q", b)], [pSk])

                            def e_fn(cell=cell):
                                pS, pSk = cell["pS"]
                                pt_, ptk = bt_new()
                                cell["pt"] = (pt_, ptk)
                                act(pt_, pS, AF.Exp, [pSk], [ptk], scale=SC_MLA)

                            def p_fn(cell=cell, kbi=kbi, first=first, last=last):
                                pt_, ptk = cell["pt"]
                                mm(psa[3], slot[:, 2 * CH + kbi * 128:2 * CH + (kbi + 1) * 128], pt_, first, last,
                                   [sk, ptk], [("ps", 3)])
                                mm(psa[4], onesb[:], pt_, first, last, ["onesb", ptk], [("ps", 4)])
                            pipe.push((s_fn, e_fn, p_fn))
                        if ci == nch - 1:
                            pipe.flush()
                            rc, rck = ft_new()
                            P.op("dve", lambda e, rc=rc: e.reciprocal(out=rc, in_=psa[4]), reads=[("ps", 4)], writes=[rck])
                            tt("dve", bg(16 + h), psa[3], rc, ALU.mult, [("ps", 3), rck], [("big", 16 + h)])
                    add_step(fn, loads, att=True)

            for h in range(8):
                pipe = Pipe()
                for ci in range(nch):
                    kt0 = kt_base + ci * CH
                    chg = kt0 // CH
                    loads = [
                        (lambda slot: slot[:, 0:CH], KDT_d[h, :, kt0:kt0 + CH], []),
                        (lambda slot: slot[:, CH:2 * CH], VD_d[h, chg].rearrange("p k d -> p (k d)"), []),
                    ]

                    def fn(slot, sk, h=h, ci=ci, pipe=pipe):
                        b = h % 2
                        if ci == 0:
                            if h == 0:
                                state["psl"] = [0, 1, 2, 7]
                                P.dma(qbuf[:, 0, 0:T], QDT_d[0, :, o0:o0 + T], writes=[("q", 0)])
                            if h + 1 < 8:
                                P.dma(qbuf[:, 1 - b, 0:T], QDT_d[h + 1, :, o0:o0 + T], writes=[("q", 1 - b)])
                            P.dma(biasb[:], biasD_d[h], writes=["biasb"])
                            P.dma(farb[:], farbD_d[h], writes=["farb"])
                        qd = qbuf[:, b, 0:T]
                        for kbi in range(NKB_CH):
                            kb = ci * NKB_CH + kbi
                            first, last = (kb == 0), (kb == nkb - 1)
                            kind = bias_kind(kb)
                            cell = {}

                            def s_fn(cell=cell, kbi=kbi):
                                for m in range(2):
                                    pS, pSk = ps_new()
                                    cell["pS", m] = (pS, pSk)
                                    mm(pS, slot[m * 64:(m + 1) * 64, kbi * 128:(kbi + 1) * 128], qd[m * 64:(m + 1) * 64, :],
                                       True, True, [sk, ("q", b)], [pSk])

                            def e_fn(cell=cell, kind=kind):
                                for m in range(2):
                                    pS, pSk = cell["pS", m]
                                    pt_, ptk = bt_new()
                                    cell["pt", m] = (pt_, ptk)
                                    if kind[0] == "far":
                                        act(pt_, pS, AF.Exp, [pSk, "farb"], [ptk], scale=SC_DIF,
                                            bias=farb[:, kind[1]:kind[1] + 1])
                                    else:
                                        tmp, tk = ft_new()
                                        stt(tmp, pS, SC_DIF, biasb[:, kind[1]:kind[1] + T], ALU.mult, ALU.add,
                                            [pSk, "biasb"], [tk])
                                        act(pt_, tmp, AF.Exp, [tk], [ptk])

                            def p_fn(cell=cell, kbi=kbi, first=first, last=last):
                                for m in range(2):
                                    pt_, ptk = cell["pt", m]
                                    mm(psa[3 + 2 * m], slot[:, CH + kbi * 128:CH + (kbi + 1) * 128], pt_, first, last,
                                       [sk, ptk], [("ps", 3 + 2 * m)])
                                    mm(psa[4 + 2 * m], onesb[:], pt_, first, last, ["onesb", ptk], [("ps", 4 + 2 * m)])
                            pipe.push((s_fn, e_fn, p_fn))
                        if ci == nch - 1:
                            pipe.flush()
                            rc0, k0_ = ft_new()
                            P.op("dve", lambda e, rc0=rc0: e.reciprocal(out=rc0, in_=psa[4]), reads=[("ps", 4)], writes=[k0_])
                            t0, tk0 = ft_new()
                            tt("dve", t0, psa[3], rc0, ALU.mult, [("ps", 3), k0_], [tk0])
                            rc1, k1_ = ft_new()
                            P.op("dve", lambda e, rc1=rc1: e.reciprocal(out=rc1, in_=psa[6]), reads=[("ps", 6)], writes=[k1_])
                            tsc("dve", rc1, rc1, small[:, 2:3], None, ALU.mult, None, [k1_, "neglam"], [k1_])
                            t1, tk1 = ft_new()
                            tt("dve", t1, psa[5], rc1, ALU.mult, [("ps", 5), k1_], [tk1])
                            ob, obk = ft_new()
                            tt("dve", ob, t0, t1, ALU.add, [tk0, tk1], [obk])
                            sq, sqk = bt_new()
                            tt("dve", sq, ob, ob, ALU.mult, [obk], [sqk])
                            p2, p2k = ps_new()
                            mm(p2, onesb[:], sq, True, True, ["onesb", sqk], [p2k])
                            rs, rk = rstd_from(p2, p2k, 128)
                            stt(bg(24 + h), ob, gc[:, GC_SUB:GC_SUB + 1], rs, ALU.mult, ALU.mult, [obk, "gc", rk], [("big", 24 + h)])
                            if h == 7:
                                state["psl"] = list(range(7))
                    add_step(fn, loads, att=True)

            def c_norm(slot, sk):
                norm_T(GC_MIX)
                transposes(GC_MIX)
            add_step(c_norm)

            for s in range(4):
                cellg = {}

                def c_ga(slot, sk, s=s, cellg=cellg):
                    for c in range(4):
                        pp, pk = ps_new()
                        gemm_fm(pp, pk, slot, sk, 16, 512, c * 128, 128, hb, HB_KEYS)
                        f, fk = ft_new()
                        act(f, pp, AF.Sigmoid, [pk], [fk])
                        cellg["a", c] = (f, fk)
                add_step(c_ga, slab_load("ga", s, 16 * 512))

                def c_upa(slot, sk, s=s, cellg=cellg):
                    for c in range(4):
                        pp, pk = ps_new()
                        gemm_fm(pp, pk, slot, sk, 8, 512, c * 128, 128, lambda kc: bg(16 + kc), [("big", 16 + i) for i in range(8)])
                        f, fk = cellg["a", c]
                        tt("dve", f, pp, f, ALU.mult, [pk, fk], [fk])
                add_step(c_upa, slab_load("upa", s, 8 * 512))

                def c_gb(slot, sk, s=s, cellg=cellg):
                    for c in range(4):
                        pp, pk = ps_new()
                        gemm_fm(pp, pk, slot, sk, 16, 512, c * 128, 128, hb, HB_KEYS)
                        f, fk = ft_new()
                        act(f, pp, AF.Sigmoid, [pk], [fk])
                        cellg["b", c] = (f, fk)
                add_step(c_gb, slab_load("gb", s, 16 * 512))

                def c_upd(slot, sk, s=s, cellg=cellg):
                    for c in range(4):
                        pp, pk = ps_new()
                        gemm_fm(pp, pk, slot, sk, 8, 512, c * 128, 128, lambda kc: bg(24 + kc), [("big", 24 + i) for i in range(8)])
                        f, fk = cellg["b", c]
                        tt("dve", f, pp, f, ALU.mult, [pk, fk], [fk])
                        fa, fak = cellg["a", c]
                        tt("dve", bg(4 * s + c), fa, f, ALU.add, [fak, fk], [("big", 4 * s + c)])
                add_step(c_upd, slab_load("upd", s, 8 * 512))

            MK = [("big", i) for i in range(16)]
            for n in range(4):
                def c_wo(slot, sk, n=n):
                    for t in range(4):
                        pp, pk = ps_new()
                        for kc in range(16):
                            mm(pp, bg(kc)[:, t * 128:(t + 1) * 128], slot[:, kc * 512:(kc + 1) * 512], kc == 0, kc == 15,
                               [sk, MK[kc]], [pk])
                        xs_ = xb(t)[:, n * 512:(n + 1) * 512]
                        tt("dve", xs_, pp, xs_, ALU.add, [pk, ("x", t)], [("x", t)])
                add_step(c_wo, slab_load("wo", n, 16 * 512))

            def f_norm(slot, sk):
                norm_T(GC_FFN)
                transposes(GC_FFN)
            add_step(f_norm)

            for s in range(11):
                cellf = {}

                def f_g(slot, sk, s=s, cellf=cellf):
                    for c in range(4):
                        pp, pk = ps_new()
                        gemm_fm(pp, pk, slot, sk, 16, 512, c * 128, 128, hb, HB_KEYS)
                        f, fk = ft_new()
                        act(f, pp, AF.Silu, [pk], [fk])
                        cellf[c] = (f, fk)
                add_step(f_g, slab_load("wg", s, 16 * 512))

                def f_u(slot, sk, s=s, cellf=cellf):
                    for c in range(4):
                        pp, pk = ps_new()
                        gemm_fm(pp, pk, slot, sk, 16, 512, c * 128, 128, hb, HB_KEYS)
                        f, fk = cellf[c]
                        tt("dve", bg(4 * s + c), pp, f, ALU.mult, [pk, fk], [("big", 4 * s + c)])
                add_step(f_u, slab_load("wu", s, 16 * 512))

            for n in range(4):
                for part in range(4):
                    def f_d(slot, sk, n=n, part=part):
                        for t in range(4):
                            for kc in range(11):
                                ch = part * 11 + kc
                                mm(psa[t], bg(ch)[:, t * 128:(t + 1) * 128], slot[:, kc * 512:(kc + 1) * 512],
                                   part == 0 and kc == 0, part == 3 and kc == 10, [sk, ("big", ch)], [("ps", t)])
                        if part == 3:
                            for t in range(4):
                                xs_ = xb(t)[:, n * 512:(n + 1) * 512]
                                tt("dve", xs_, psa[t], xs_, ALU.add, [("ps", t), ("x", t)], [("x", t)])
                            if n == 3:
                                for t in range(4):
                                    P.dma(ysrc[t * 128:(t + 1) * 128, :], xb(t), reads=[("x", t)])
                    add_step(f_d, slab_load("wd", n * 4 + part, 11 * 512))

        if stage >= 4:
            for j in range(ngp):
                own_group("p", j)
        if stage >= 4:
            for j in range(ngs):
                own_group("s", j)

        load_steps = [i for i, (fn, l, a) in enumerate(steps) if l]
        issued = [0]

        def issue_upto(m):
            while issued[0] <= m and issued[0] < len(load_steps):
                idx = load_steps[issued[0]]
                si = issued[0] % 3
                for (dst_fn, src, rk) in steps[idx][1]:
                    P.dma(dst_fn(wslot[si]), src, reads=rk, writes=[("wslot", si)])
                issued[0] += 1

        nexec = 0
        for i, (fn, loads, att) in enumerate(steps):
            if max_steps is not None and i >= max_steps:
                break
            if loads:
                prev_att = nexec > 0 and steps[load_steps[nexec - 1]][2]
                issue_upto(nexec + (1 if prev_att else 2))
                si = nexec % 3
                fn(wslot[si], ("wslot", si))
                nexec += 1
            else:
                fn(None, None)

    return nc, P, st, body


_CACHE = {}


def run(inputs, S_P, S_S, debug=False, stage=9, max_steps=None, ncores=NCORES, first=0):
    key = (S_P, S_S, debug, stage, max_steps)
    if key not in _CACHE:
        _CACHE[key] = build_program(S_P, S_S, debug, stage, max_steps)
    nc = _CACHE[key]
    maps = host_prep(inputs, S_P, S_S)
    res = run_bass_kernel_spmd(nc, maps[first:first + ncores], core_ids=list(range(ncores)))
    return res


SINGLE_LAUNCH = True


def kernel(**inputs):
    S_P = int(np.asarray(inputs["x_prompt"]).shape[1])
    S_S = int(np.asarray(inputs["x_sample"]).shape[1])
    NPC = S_P // NCORES
    yp = np.empty((1, S_P, D_MODEL), np.float32)
    ys = np.empty((NCORES, S_S, D_MODEL), np.float32)
    if SINGLE_LAUNCH:
        res = run(inputs, S_P, S_S)
        for c in range(NCORES):
            yp[0, c * NPC:(c + 1) * NPC] = np.asarray(res.results[c]["yp"], dtype=np.float32)
            ys[c] = np.asarray(res.results[c]["ys"], dtype=np.float32)
        return (yp, ys)
    key = (S_P, S_S, False, 9, None)
    if key not in _CACHE:
        _CACHE[key] = build_program(S_P, S_S)
    nc = _CACHE[key]
    maps = host_prep(inputs, S_P, S_S)
    for c in range(NCORES):
        res = run_bass_kernel_spmd(nc, [maps[c]], core_ids=[0])
        yp[0, c * NPC:(c + 1) * NPC] = np.asarray(res.results[0]["yp"], dtype=np.float32)
        ys[c] = np.asarray(res.results[0]["ys"], dtype=np.float32)
        maps[c] = None
    return (yp, ys)
```

```python
import bisect
import contextlib
import math
import numpy as np
import concourse.bass as bass
import concourse.mybir as mybir
from concourse.bass_utils import run_bass_kernel_spmd

F32 = mybir.dt.float32
BF16 = mybir.dt.bfloat16
ALU = mybir.AluOpType
AF = mybir.ActivationFunctionType
AX = mybir.AxisListType

D_MODEL = 2048
MLA_HEADS = 8
Q_LORA = 512
KV_LORA = 256
D_FF = 5632
EPS = 1e-6
LAM_INIT = 0.8 - 0.6 * math.exp(0.0)
NCORES = 8
CAST_BARRIER = True
T = 512
NBW = 1152
LW = 1279


class Prog:
    COMPUTE = ("pe", "act", "dve", "pool")

    def __init__(self, nc, nsp=8, npool=4):
        self.nc = nc
        self.q = {e: [] for e in ("pe", "act", "dve", "pool", "sp")}
        self.sems = {}
        self.sem_ctx = []
        for e in self.COMPUTE:
            self.sems[e] = self._sem("c_" + e)
        self.dq = {"sp": [self._sem(f"dsp{i}") for i in range(nsp)],
                   "pool": [self._sem(f"dpl{i}") for i in range(npool)]}
        self.dn = {"sp": 0, "pool": 0}
        self.nops = {e: 0 for e in self.COMPUTE}
        self.inc_idx = {e: [] for e in self.COMPUTE}
        self.ents = {e: {} for e in self.COMPUTE}
        self.seen = {e: {} for e in self.q}
        self.last_w = {}
        self.readers = {}

    def _sem(self, name):
        ctx = self.nc.semaphore(name)
        s = ctx.__enter__()
        self.sem_ctx.append(ctx)
        return s

    def _need(self, eng, tok, waits):
        if tok is None:
            return
        if tok[0] == "c":
            _, e, idx = tok
            if e == eng and e == "pe":
                return
            lst = self.inc_idx[e]
            p = bisect.bisect_left(lst, idx)
            if p < len(lst):
                val = p + 1
            else:
                self.ents[e][idx]["inc"] = True
                lst.append(idx)
                val = len(lst)
            sem = self.sems[e]
        else:
            _, sem, val = tok
        sid = id(sem)
        if self.seen[eng].get(sid, 0) >= val:
            return
        self.seen[eng][sid] = val
        waits.append((sem, val))

    @staticmethod
    def _chan(tok):
        return tok[1] if tok[0] == "c" else id(tok[1])

    def _deps(self, eng, reads, writes, is_dma):
        need = {}

        def add(t):
            if t is None:
                return
            if (not is_dma) and t[0] == "c" and t[1] == eng and eng == "pe":
                return
            c = self._chan(t)
            o = need.get(c)
            if o is None or t[2] > o[2]:
                need[c] = t

        for k in reads:
            add(self.last_w.get(k))
        for k in writes:
            t = self.last_w.get(k)
            if t is not None and (is_dma or not (t[0] == "c" and t[1] == eng)):
                add(t)
            for r in self.readers.get(k, {}).values():
                if is_dma or not (r[0] == "c" and r[1] == eng):
                    add(r)
        waits = []
        for t in need.values():
            self._need(eng, t, waits)
        return waits

    def _commit(self, tok, reads, writes):
        c = self._chan(tok)
        for k in reads:
            d = self.readers.setdefault(k, {})
            o = d.get(c)
            if o is None or tok[2] > o[2]:
                d[c] = tok
        for k in writes:
            self.last_w[k] = tok
            self.readers[k] = {}

    def op(self, eng, fn, reads=(), writes=()):
        waits = self._deps(eng, reads, writes, False)
        self.nops[eng] += 1
        idx = self.nops[eng]
        ent = {"fn": fn, "waits": waits, "inc": False}
        self.ents[eng][idx] = ent
        self.q[eng].append(ent)
        tok = ("c", eng, idx)
        self._commit(tok, reads, writes)
        return tok

    def dma(self, out, in_, reads=(), writes=(), queue="sp"):
        waits = self._deps(queue, reads, writes, True)
        pool = self.dq[queue]
        n = self.dn[queue]
        self.dn[queue] += 1
        sem = pool[n % len(pool)]
        rnd = n // len(pool)
        if rnd > 0:
            self._need(queue, ("d", sem, 16 * rnd), waits)
        tok = ("d", sem, 16 * (rnd + 1))
        ent = {"fn": (lambda e, o=out, i=in_: e.dma_start(out=o, in_=i)),
               "waits": waits, "dma": sem}
        self.q[queue].append(ent)
        self._commit(tok, reads, writes)
        return tok

    def drain(self, eng, queues=("sp", "pool")):
        waits = []
        for qn in queues:
            pool = self.dq[qn]
            n = self.dn[qn]
            for i, sem in enumerate(pool):
                cnt = (n - i + len(pool) - 1) // len(pool) if n > i else 0
                if cnt > 0:
                    self._need(eng, ("d", sem, 16 * cnt), waits)
        if waits:
            self.q[eng].append({"fn": None, "waits": waits})

    def emit(self):
        nc = self.nc
        handles = {"pe": "tensor", "act": "scalar", "dve": "vector", "pool": "gpsimd", "sp": "sync"}
        with nc.Block() as block:
            for ename, attr in handles.items():
                ents = self.q[ename]
                if not ents:
                    continue
                csem = self.sems.get(ename)

                def body(eng, ents=ents, csem=csem):
                    for ent in ents:
                        for (s, v) in ent["waits"]:
                            eng.wait_ge(s, v)
                        if ent["fn"] is None:
                            continue
                        ins = ent["fn"](eng)
                        if "dma" in ent:
                            ins.then_inc(ent["dma"], 16)
                        elif ent["inc"]:
                            ins.then_inc(csem, 1)
                getattr(block, attr)(body)
        for ctx in reversed(self.sem_ctx):
            ctx.__exit__(None, None, None)

def t5_bucket_np(rel):
    nb = 16
    max_exact = 8
    rel = np.asarray(rel, dtype=np.int64)
    ret = np.where(rel > 0, nb, 0)
    n = np.abs(rel)
    nf = np.maximum(n, 1).astype(np.float32)
    ratio = np.log(nf / np.float32(max_exact)) / np.float32(math.log(128 / max_exact))
    large = max_exact + (ratio.astype(np.float32) * np.float32(nb - max_exact)).astype(np.int32)
    large = np.minimum(large, nb - 1)
    return (ret + np.where(n < max_exact, n, large)).astype(np.int64)


def onehot32(b):
    return (np.arange(32)[:, None] == np.asarray(b).reshape(1, -1)).astype(np.float32)


def rope_tables(pos):
    half = 32
    inv = (10000.0 ** (-np.arange(half, dtype=np.float32) / half)).astype(np.float32)
    ang = pos.astype(np.float32)[None, :] * inv[:, None]
    cos = np.cos(ang).astype(np.float32)
    sin = np.sin(ang).astype(np.float32)
    c = np.concatenate([cos, cos], axis=0)
    s = np.concatenate([-sin, sin], axis=0)
    return np.ascontiguousarray(c), np.ascontiguousarray(s)


def swap_halves(a, axis=-1):
    h = a.shape[axis] // 2
    lo = np.take(a, np.arange(0, h), axis=axis)
    hi = np.take(a, np.arange(h, 2 * h), axis=axis)
    return np.concatenate([hi, lo], axis=axis)


def host_prep(inp, S_P, S_S):
    NPC = S_P // NCORES
    f = lambda a: np.ascontiguousarray(np.asarray(a, dtype=np.float32))
    xP = f(inp["x_prompt"])[0]
    xS = f(inp["x_sample"])
    w_in = f(inp["w_in"])[0]
    wq_b = f(inp["wq_b"])[0]
    shared = {
        "w_in": w_in,
        "w_kpeP": np.ascontiguousarray(swap_halves(w_in[:, 768:832])),
        "wq_b": wq_b,
        "wq_ropeP": np.ascontiguousarray(
            swap_halves(wq_b.reshape(512, 8, 192)[:, :, 128:192]).reshape(512, 512)),
        "wkv_b": f(inp["wkv_b"])[0],
        "w_up_mla": f(inp["w_up_mla"])[0],
        "w_up_diff": f(inp["w_up_diff"])[0],
        "w_o": f(inp["w_o"])[0],
        "w_gate": f(inp["w_gate"])[0],
        "w_up": f(inp["w_up"])[0],
        "w_down": f(inp["w_down"])[0],
        "relb": f(inp["rel_bias"]),
        "ident": np.eye(128, dtype=np.float32),
    }
    gc = np.zeros((128, 48), np.float32)
    gc[:, 0:16] = f(inp["mix_norm"])[0].reshape(16, 128).T
    gc[:, 16:32] = f(inp["ffn_norm"])[0].reshape(16, 128).T
    gc[:, 32:36] = f(inp["q_a_norm"])[0].reshape(4, 128).T
    gc[:, 36:38] = f(inp["kv_a_norm"])[0].reshape(2, 128).T
    mq = f(inp["mla_q_norm"])[0]
    mk = f(inp["mla_k_norm"])[0]
    gc[:, 38] = mq[0:128]
    gc[0:64, 39] = mq[128:192]
    gc[0:64, 40] = swap_halves(mq[128:192])
    gc[:, 41] = mk[0:128]
    gc[0:64, 42] = mk[128:192]
    gc[0:64, 43] = swap_halves(mk[128:192])
    gc[:, 44] = np.tile(f(inp["diff_q_norm"])[0], 2)
    gc[:, 45] = np.tile(f(inp["diff_k_norm"])[0], 2)
    gc[:, 46] = f(inp["diff_subln"])[0]
    shared["gcols"] = gc
    lam = np.stack([f(inp["lambda_q1"])[0], f(inp["lambda_k1"])[0],
                    f(inp["lambda_q2"])[0], f(inp["lambda_k2"])[0]], 0).reshape(1, 256)
    shared["lamv"] = np.ascontiguousarray(np.tile(lam, (128, 1)))
    shared["ohw"] = onehot32(t5_bucket_np(639 - np.arange(LW)))

    ngp, nkp = NPC // T, S_P // 128
    ngs, nks = S_S // T, S_S // 128
    maps = []
    for c in range(NCORES):
        m = dict(shared)
        m["xp"] = np.ascontiguousarray(np.roll(xP, -c * NPC, axis=0))
        m["xs"] = np.ascontiguousarray(xS[c])
        posP = (np.arange(S_P) + c * NPC) % S_P
        posS = np.arange(S_S)
        cp, sp = rope_tables(posP)
        cs, ss = rope_tables(posS)
        m["ropeC"] = np.ascontiguousarray(np.concatenate([cp, cs], 1))
        m["ropeS"] = np.ascontiguousarray(np.concatenate([sp, ss], 1))
        cols = []
        for j in range(ngp):
            qlo, qhi = posP[j * T], posP[j * T + T - 1]
            for kb in range(nkp):
                klo = posP[kb * 128]
                rel = (klo - qhi) if klo > qhi else (klo + 127 - qlo)
                cols.append(int(t5_bucket_np(rel)))
        for j in range(ngs):
            for kb in range(nks):
                klo = kb * 128
                rel = (klo - (j * T + T - 1)) if klo > j * T + T - 1 else (klo + 127 - j * T)
                cols.append(int(t5_bucket_np(rel)))
        cols += [15, 31]
        m["farsel"] = onehot32(np.array(cols))
        selL = 1.0 if c > 0 else 0.0
        selR = 1.0 if c < NCORES - 1 else 0.0
        m["sel"] = np.ascontiguousarray(
            np.tile(np.array([[selL, 1 - selL, selR, 1 - selR]], np.float32), (128, 1)))
        maps.append(m)
    return maps

class _Stop(Exception):
    pass


def build_program(S_P, S_S, debug=False, stage=9, max_steps=None):
    nc, P, st, body = _build_body(S_P, S_S, debug, stage, max_steps)
    try:
        body()
    except _Stop:
        pass
    P.drain("sp", queues=("sp", "pool"))
    P.emit()
    st.close()
    return nc


def _build_body(S_P, S_S, debug, stage, max_steps=None):
    NPC = S_P // NCORES
    NKEY = S_P + S_S
    NOWN = NPC + S_S
    CH = min(2048, S_S)
    NKB_CH = CH // 128
    ngp, nkp = NPC // T, S_P // 128
    ngs, nks = S_S // T, S_S // 128
    NF = ngp * nkp + ngs * nks + 2
    assert NF <= 1024 and NPC % T == 0 and S_S % T == 0 and S_P % CH == 0

    nc = bass.Bass("TRN2", target_bir_lowering=False)
    P = Prog(nc)

    def din(name, shape):
        return nc.dram_tensor(name, list(shape), F32, kind="ExternalInput")

    def dscr(name, shape, dt=BF16):
        return nc.dram_tensor(name, list(shape), dt, kind="Internal")

    xp_d = din("xp", [S_P, D_MODEL]).ap()
    xs_d = din("xs", [S_S, D_MODEL]).ap()
    w_in_d = din("w_in", [2048, 8000]).ap()
    w_kpeP_d = din("w_kpeP", [2048, 64]).ap()
    wq_b_d = din("wq_b", [512, 1536]).ap()
    wq_ropeP_d = din("wq_ropeP", [512, 512]).ap()
    wkv_b_d = din("wkv_b", [256, 2048]).ap()
    w_up_mla_d = din("w_up_mla", [1024, 2048]).ap()
    w_up_diff_d = din("w_up_diff", [1024, 2048]).ap()
    w_o_d = din("w_o", [2048, 2048]).ap()
    w_gate_d = din("w_gate", [2048, D_FF]).ap()
    w_up_d = din("w_up", [2048, D_FF]).ap()
    w_down_d = din("w_down", [D_FF, 2048]).ap()
    relb_d = din("relb", [32, 8]).ap()
    ident_d = din("ident", [128, 128]).ap()
    gcols_d = din("gcols", [128, 48]).ap()
    lamv_d = din("lamv", [128, 256]).ap()
    ohw_d = din("ohw", [32, LW]).ap()
    ropeC_d = din("ropeC", [64, NKEY]).ap()
    ropeS_d = din("ropeS", [64, NKEY]).ap()
    farsel_d = din("farsel", [32, NF]).ap()
    sel_d = din("sel", [128, 4]).ap()
    yp_d = nc.dram_tensor("yp", [NPC, D_MODEL], F32, kind="ExternalOutput").ap()
    ys_d = nc.dram_tensor("ys", [S_S, D_MODEL], F32, kind="ExternalOutput").ap()

    slabs = {}

    def mkslab(name, n, kc, w):
        slabs[name] = dscr("s_" + name, [n, 128, kc, w]).ap()

    mkslab("cq", 1, 16, 512); mkslab("ckv", 1, 16, 256); mkslab("kpe", 1, 16, 128)
    mkslab("dq", 2, 16, 512); mkslab("dk", 2, 16, 512); mkslab("dv", 2, 16, 512)
    mkslab("ga", 4, 16, 512); mkslab("gb", 4, 16, 512)
    mkslab("wqb", 1, 4, 2048); mkslab("wkvk", 1, 2, 1024); mkslab("wkvv", 1, 2, 1024)
    mkslab("upa", 4, 8, 512); mkslab("upd", 4, 8, 512); mkslab("wo", 4, 16, 512)
    mkslab("wg", 11, 16, 512); mkslab("wu", 11, 16, 512); mkslab("wd", 16, 11, 512)

    kind_dbg = "ExternalOutput" if debug else "Internal"

    def dscr2(name, shape, dt=BF16):
        return nc.dram_tensor(name, list(shape), dt, kind=kind_dbg)

    KTn_d = dscr2("KTn", [8, 128, NKEY]).ap()
    KTp_d = dscr2("KTp", [8, 64, NKEY]).ap()
    KDT_d = dscr2("KDT", [8, 128, NKEY]).ap()
    VA_d = dscr2("VA", [8, NKEY // CH, 128, NKB_CH, 128]).ap()
    VD_d = dscr2("VD", [8, NKEY // CH, 128, NKB_CH, 128]).ap()
    QTn_d = dscr2("QTn", [8, 128, NOWN]).ap()
    QTp_d = dscr2("QTp", [8, 64, NOWN]).ap()
    QDT_d = dscr2("QDT", [8, 128, NOWN]).ap()
    rrep_t = dscr("rrep", [8, 128, LW], F32)
    rrep_d = rrep_t.ap()
    biasD_d = dscr2("biasD", [8, 128, NBW + 1024], F32).ap()
    farbD_d = dscr2("farbD", [8, 128, NF], F32).ap()

    st = contextlib.ExitStack()

    def sb(name, shape, dt):
        return st.enter_context(nc.sbuf_tensor(name, list(shape), dt))

    def pst(name, shape, dt):
        return st.enter_context(nc.psum_tensor(name, list(shape), dt))

    xbuf = sb("xbuf", [128, 4 * 2048], F32)
    hbuf = sb("hbuf", [128, 16 * T], BF16)
    big = sb("big", [128, 48 * T], BF16)
    wslot = [sb(f"wslot{i}", [128, 8192], BF16) for i in range(3)]
    ftmp = sb("ftmp", [128, 8, T], F32)
    btmp = sb("btmp", [128, 6, T], BF16)
    qbuf = sb("qbuf", [128, 2, 1024], BF16)
    biasb = sb("biasb", [128, NBW + 1024], F32)
    farb = sb("farb", [128, NF], F32)
    gc = sb("gc", [128, 48], F32)
    identb = sb("identb", [128, 128], BF16)
    onesb = sb("onesb", [128, 128], BF16)
    oblk = sb("oblk", [128, 128], BF16)
    ones32 = sb("ones32", [32, 128], F32)
    relbT = sb("relbT", [32, 8], F32)
    selt = sb("selt", [128, 4], F32)
    lamt = sb("lamt", [128, 256], F32)
    small = sb("small", [128, 32], F32)
    ropec = sb("ropec", [64, T], F32)
    ropes = sb("ropes", [64, T], F32)
    kperot = sb("kperot", [64, T], F32)
    stat = sb("stat", [128, 16], F32)
    sqkpe = sb("sqkpe", [64, T], BF16)
    psb = [pst(f"ps{i}", [128, T], F32) for i in range(8)]
    psa = [p[:] for p in psb]
    ptr = psa[7].bitcast(BF16)
    assert tuple(ptr.shape) == (128, 2 * T), ptr.shape

    state = {"ps": 0, "psl": list(range(7)), "ft": 0, "bt": 0, "pt": 0}

    def ps_new():
        l = state["psl"]
        i = l[state["ps"] % len(l)]
        state["ps"] += 1
        return psa[i], ("ps", i)

    def ft_new():
        i = state["ft"] % 8
        state["ft"] += 1
        return ftmp[:, i, :], ("ft", i)

    def bt_new():
        i = state["bt"] % 6
        state["bt"] += 1
        return btmp[:, i, :], ("bt", i)

    def pt_new():
        i = state["pt"] % 2
        state["pt"] += 1
        return ptr[:, i * T:(i + 1) * T], ("ps", 7)

    def mm(out, lhsT, rhs, start, stop, reads, writes):
        P.op("pe", lambda e: e.matmul(out, lhsT=lhsT, rhs=rhs, start=start, stop=stop),
             reads=reads, writes=writes)

    def act(out, in_, func, reads, writes, **kw):
        P.op("act", lambda e: e.activation(out=out, in_=in_, func=func, **kw),
             reads=reads, writes=writes)

    def tsc(eng, out, in0, s1, s2, op0, op1, reads, writes):
        if s2 is None:
            P.op(eng, lambda e: e.tensor_scalar(out=out, in0=in0, scalar1=s1, scalar2=None, op0=op0),
                 reads=reads, writes=writes)
        else:
            P.op(eng, lambda e: e.tensor_scalar(out=out, in0=in0, scalar1=s1, scalar2=s2,
                                                op0=op0, op1=op1), reads=reads, writes=writes)

    def tt(eng, out, in0, in1, op, reads, writes):
        P.op(eng, lambda e: e.tensor_tensor(out=out, in0=in0, in1=in1, op=op),
             reads=reads, writes=writes)

    def stt(out, in0, s, in1, op0, op1, reads, writes):
        P.op("dve", lambda e: e.scalar_tensor_tensor(out=out, in0=in0, scalar=s, in1=in1,
                                                      op0=op0, op1=op1), reads=reads, writes=writes)

    def rstd_from(ps_ap, ps_key, dim, npart=128):
        t1, k1 = ft_new()
        act(t1[:npart], ps_ap[:npart], AF.Sqrt, [ps_key], [k1], scale=1.0 / dim, bias=EPS)
        t2, k2 = ft_new()
        P.op("dve", lambda e: e.reciprocal(out=t2[:npart], in_=t1[:npart]), reads=[k1], writes=[k2])
        return t2, k2

    GC_MIX, GC_FFN, GC_QA, GC_KVA = 0, 16, 32, 36
    GC_MQN, GC_MQR, GC_MQRP, GC_MKN, GC_MKR, GC_MKRP, GC_DQ, GC_DK, GC_SUB = 38, 39, 40, 41, 42, 43, 44, 45, 47

    def body():
        P.dma(gc[:], gcols_d, writes=["gc"])
        P.dma(ftmp[:, 0, 0:128], ident_d, writes=[("ft", 0)])
        P.dma(lamt[:], lamv_d, writes=["lamt"])
        P.dma(relbT[:], relb_d, writes=["relbT"])
        P.dma(selt[:], sel_d, writes=["selt"])
        P.op("dve", lambda e: e.tensor_copy(out=identb[:], in_=ftmp[:, 0, 0:128]), reads=[("ft", 0)], writes=["identb"])
        P.op("dve", lambda e: e.memset(onesb[:], 1.0), writes=["onesb"])
        P.op("dve", lambda e: e.memset(oblk[:], 0.0), writes=["oblk"])
        P.op("dve", lambda e: e.memset(oblk[0:64, 0:64], 1.0), writes=["oblk"])
        P.op("dve", lambda e: e.memset(oblk[64:128, 64:128], 1.0), writes=["oblk"])
        P.op("dve", lambda e: e.memset(ones32[:], 1.0), writes=["ones32"])
        tt("dve", ftmp[:, 1, 0:64], lamt[:, 0:64], lamt[:, 64:128], ALU.mult, ["lamt"], [("ft", 1)])
        tt("dve", ftmp[:, 1, 64:128], lamt[:, 128:192], lamt[:, 192:256], ALU.mult, ["lamt"], [("ft", 1)])
        P.op("dve", lambda e: e.reduce_sum(out=small[:, 0:1], in_=ftmp[:, 1, 0:64], axis=AX.X), reads=[("ft", 1)], writes=["sm0"])
        P.op("dve", lambda e: e.reduce_sum(out=small[:, 1:2], in_=ftmp[:, 1, 64:128], axis=AX.X), reads=[("ft", 1)], writes=["sm1"])
        act(small[:, 3:4], small[:, 0:1], AF.Exp, ["sm0"], ["sm3"])
        act(small[:, 4:5], small[:, 1:2], AF.Exp, ["sm1"], ["sm4"])
        tt("dve", small[:, 5:6], small[:, 4:5], small[:, 3:4], ALU.subtract, ["sm3", "sm4"], ["sm5"])
        tsc("dve", small[:, 2:3], small[:, 5:6], -LAM_INIT, None, ALU.add, None, ["sm5"], ["neglam"])
        tsc("dve", gc[:, 47:48], gc[:, 46:47], 1.0 - LAM_INIT, None, ALU.mult, None, ["gc"], ["gc"])

        if stage == 0:
            raise _Stop
        def cast(dst, src, key):
            P.dma(dst, src, writes=[key], queue="pool")

        def cast_cols(name, src, c0, nslab, w):
            for s in range(nslab):
                cast(slabs[name][s], src[:, c0 + s * w:c0 + (s + 1) * w].rearrange("(kc p) n -> p kc n", p=128),
                     ("slab", name, s))

        cast_cols("ckv", w_in_d, 512, 1, 256)
        cast(slabs["kpe"][0][:, :, 0:64], w_in_d[:, 768:832].rearrange("(kc p) n -> p kc n", p=128), ("slab", "kpe", 0))
        cast(slabs["kpe"][0][:, :, 64:128], w_kpeP_d.rearrange("(kc p) n -> p kc n", p=128), ("slab", "kpe", 0))
        for kc in range(2):
            src = wkv_b_d[kc * 128:(kc + 1) * 128, :].rearrange("p (h e) -> p h e", e=256)
            cast(slabs["wkvk"][0][:, kc, :].rearrange("p (h d) -> p h d", d=128), src[:, :, 0:128], ("slab", "wkvk", 0))
            cast(slabs["wkvv"][0][:, kc, :].rearrange("p (h d) -> p h d", d=128), src[:, :, 128:256], ("slab", "wkvv", 0))
        cast_cols("dk", w_in_d, 1856, 2, 512)
        cast_cols("dv", w_in_d, 2880, 2, 512)
        cast_cols("cq", w_in_d, 0, 1, 512)
        for kc in range(4):
            dst = slabs["wqb"][0][:, kc, :].rearrange("p (h d) -> p h d", d=256)
            cast(dst[:, :, 0:192], wq_b_d[kc * 128:(kc + 1) * 128, :].rearrange("p (h d) -> p h d", d=192), ("slab", "wqb", 0))
            cast(dst[:, :, 192:256], wq_ropeP_d[kc * 128:(kc + 1) * 128, :].rearrange("p (h d) -> p h d", d=64), ("slab", "wqb", 0))
        cast_cols("dq", w_in_d, 832, 2, 512)
        cast_cols("ga", w_in_d, 3904, 4, 512)
        cast_cols("gb", w_in_d, 5952, 4, 512)
        cast_cols("upa", w_up_mla_d, 0, 4, 512)
        cast_cols("upd", w_up_diff_d, 0, 4, 512)
        cast_cols("wo", w_o_d, 0, 4, 512)
        cast_cols("wg", w_gate_d, 0, 11, 512)
        cast_cols("wu", w_up_d, 0, 11, 512)
        for n in range(4):
            for part in range(4):
                cast(slabs["wd"][n * 4 + part],
                     w_down_d[part * 1408:(part + 1) * 1408, n * 512:(n + 1) * 512].rearrange("(kc p) n -> p kc n", p=128),
                     ("slab", "wd", n * 4 + part))

        if CAST_BARRIER:
            P.drain("sp", queues=("pool",))
        if stage == 1:
            raise _Stop
        ohw_sb = xbuf[0:32, 0:LW]
        fsel_sb = xbuf[0:32, 2048:2048 + NF]
        P.dma(ohw_sb, ohw_d, writes=["xb0"])
        P.dma(fsel_sb, farsel_d, writes=["xb1"])
        for h in range(8):
            lh = xbuf[0:32, 4096:4096 + 128]
            tsc("dve", lh, ones32[:], relbT[:, h:h + 1], None, ALU.mult, None, ["ones32", "relbT"], ["xb2"])
            wrep = xbuf[:, 6144:6144 + LW]
            for c0 in range(0, LW, 512):
                n = min(512, LW - c0)
                pp, pk = ps_new()
                mm(pp[:, 0:n], lh, ohw_sb[:, c0:c0 + n], True, True, ["xb2", "xb0"], [pk])
                act(wrep[:, c0:c0 + n], pp[:, 0:n], AF.Copy, [pk], ["xb3"])
            P.dma(rrep_d[h], wrep, reads=["xb3"])
            fr = xbuf[0:32, 5120:5120 + NF]
            tsc("dve", fr, fsel_sb, relbT[:, h:h + 1], None, ALU.mult, None, ["xb1", "relbT"], ["xb2b"])
            for c0 in range(0, NF, 512):
                n = min(512, NF - c0)
                pp, pk = ps_new()
                mm(pp[:, 0:n], ones32[:], fr[:, c0:c0 + n], True, True, ["xb2b", "ones32"], [pk])
                act(farb[:, c0:c0 + n], pp[:, 0:n], AF.Copy, [pk], ["farb"])
            P.dma(farbD_d[h], farb[:], reads=["farb"])
            P.op("dve", lambda e, h=h: e.tensor_copy(out=small[:, 8 + h:9 + h], in_=farb[:, NF - 2:NF - 1]), reads=["farb"], writes=[("f15", h)])
            P.op("dve", lambda e, h=h: e.tensor_copy(out=small[:, 16 + h:17 + h], in_=farb[:, NF - 1:NF]), reads=["farb"], writes=[("f31", h)])
        P.drain("sp", queues=("sp",))
        for h in range(8):
            P.dma(biasb[:, 0:NBW], bass.AP(rrep_t, h * 128 * LW + 127, [[LW - 1, 128], [1, NBW]]), writes=["biasb"])
            tt("dve", small[:, 24:25], selt[:, 1:2], small[:, 16 + h:17 + h], ALU.mult, ["selt", ("f31", h)], ["cL"])
            tt("dve", small[:, 25:26], selt[:, 3:4], small[:, 8 + h:9 + h], ALU.mult, ["selt", ("f15", h)], ["cR"])
            tsc("dve", biasb[:, NBW:NBW + 512], biasb[:, 640:1152], selt[:, 0:1], small[:, 24:25], ALU.mult, ALU.add,
                ["biasb", "selt", "cL"], ["biasb"])
            tsc("dve", biasb[:, NBW + 512:NBW + 1024], biasb[:, 0:512], selt[:, 2:3], small[:, 25:26], ALU.mult, ALU.add,
                ["biasb", "selt", "cR"], ["biasb"])
            P.dma(biasD_d[h], biasb[:], reads=["biasb"])
        P.drain("sp", queues=("sp",))

        if stage == 2:
            raise _Stop
        steps = []

        def add_step(fn, loads=None, att=False):
            steps.append((fn, loads, att))

        def slab_load(name, s, n):
            src = slabs[name][s].rearrange("p k w -> p (k w)")
            return [(lambda slot: slot[:, 0:n], src, [("slab", name, s)])]

        def hb(c):
            return hbuf[:, c * T:(c + 1) * T]

        def bg(c):
            return big[:, c * T:(c + 1) * T]

        def xb(t):
            return xbuf[:, t * 2048:(t + 1) * 2048]

        def xsb(t):
            return big[:, (32 + 4 * t) * T:(36 + 4 * t) * T]

        XSB_KEYS = [[("big", 32 + 4 * t + i) for i in range(4)] for t in range(4)]
        HB_KEYS = [("hb", c) for c in range(16)]

        def x_src(k0):
            return xp_d[k0:k0 + T, :] if k0 < S_P else xs_d[k0 - S_P:k0 - S_P + T, :]

        def load_x(k0):
            src = x_src(k0)
            for t in range(4):
                P.dma(xb(t), src[t * 128:(t + 1) * 128, :], writes=[("x", t)])

        def norm_T(goff):
            for t in range(4):
                act(xsb(t), xb(t), AF.Square, [("x", t)], XSB_KEYS[t] + [("st", t)], accum_out=stat[:, t:t + 1])
                act(stat[:, 4 + t:5 + t], stat[:, t:t + 1], AF.Sqrt, [("st", t)], [("st", 4 + t)], scale=1.0 / D_MODEL, bias=EPS)
                P.op("dve", lambda e, t=t: e.reciprocal(out=stat[:, 8 + t:9 + t], in_=stat[:, 4 + t:5 + t]),
                     reads=[("st", 4 + t)], writes=[("st", 8 + t)])
                tsc("dve", xsb(t), xb(t), stat[:, 8 + t:9 + t], None, ALU.mult, None,
                    [("x", t), ("st", 8 + t)], XSB_KEYS[t])

        def transposes(goff):
            for c in range(16):
                pt, pk = pt_new()
                for t in range(4):
                    P.op("pe", lambda e, t=t, c=c, pt=pt: e.transpose(out=pt[:, t * 128:(t + 1) * 128],
                                                                      in_=xsb(t)[:, c * 128:(c + 1) * 128],
                                                                      identity=identb[:]),
                         reads=XSB_KEYS[t] + ["identb"], writes=[pk])
                if c % 2 == 0:
                    tsc("dve", hb(c), pt, gc[:, goff + c:goff + c + 1], None, ALU.mult, None, [pk, "gc"], [HB_KEYS[c]])
                else:
                    P.op("act", lambda e, c=c, pt=pt: e.mul(out=hb(c), in_=pt, mul=gc[:, goff + c:goff + c + 1]),
                         reads=[pk, "gc"], writes=[HB_KEYS[c]])

        def gemm_fm(pp, pk, slot, sk, kcn, w, c0, m, rhs_fn, rhs_keys):
            for kc in range(kcn):
                mm(pp[0:m, :], slot[:, kc * w + c0:kc * w + c0 + m], rhs_fn(kc), kc == 0, kc == kcn - 1,
                   [sk] + rhs_keys, [pk])

        def square_bf(pp, pk, npart=128):
            b, bk = bt_new()
            act(b[0:npart], pp[0:npart], AF.Square, [pk], [bk])
            return b, bk

        groupsA = list(range(NKEY // T))

        def own_off(g):
            k0 = g * T
            if g < ngp:
                return k0
            if k0 >= S_P:
                return NPC + (k0 - S_P)
            return None

        def v_dst(Vd, h0, kt):
            ch, kbi = kt // CH, (kt % CH) // 128
            return Vd[h0:h0 + 4, ch, :, kbi, :].rearrange("h p d -> p h d")

        def phaseA(gi):
            g = groupsA[gi]
            k0 = g * T
            o0 = own_off(g)

            def a_norm(slot, sk):
                if gi == 0:
                    load_x(k0)
                    norm_T(GC_MIX)
                    if 1 < len(groupsA):
                        load_x(groupsA[1] * T)
                transposes(GC_MIX)
            add_step(a_norm)

            def a_prenorm(slot, sk):
                if gi + 1 < len(groupsA):
                    norm_T(GC_MIX)
                    if gi + 2 < len(groupsA):
                        load_x(groupsA[gi + 2] * T)

            def a_ckv(slot, sk):
                pps = []
                sqs = []
                for cc in range(2):
                    pp, pk = ps_new()
                    gemm_fm(pp, pk, slot, sk, 16, 256, cc * 128, 128, hb, HB_KEYS)
                    pps.append((pp, pk))
                    sqs.append(square_bf(pp, pk))
                p2, p2k = ps_new()
                for cc in range(2):
                    mm(p2, onesb[:], sqs[cc][0], cc == 0, cc == 1, ["onesb", sqs[cc][1]], [p2k])
                rs, rk = rstd_from(p2, p2k, KV_LORA)
                for cc in range(2):
                    stt(bg(cc), pps[cc][0], gc[:, GC_KVA + cc:GC_KVA + cc + 1], rs, ALU.mult, ALU.mult,
                        [pps[cc][1], "gc", rk], [("big", cc)])
            add_step(a_ckv, slab_load("ckv", 0, 16 * 256))

            def a_kpe(slot, sk):
                P.dma(ropec[:], ropeC_d[:, k0:k0 + T], writes=["ropec"])
                P.dma(ropes[:], ropeS_d[:, k0:k0 + T], writes=["ropes"])
                pr, prk = ps_new()
                gemm_fm(pr, prk, slot, sk, 16, 128, 0, 64, hb, HB_KEYS)
                prp, prpk = ps_new()
                gemm_fm(prp, prpk, slot, sk, 16, 128, 64, 64, hb, HB_KEYS)
                act(sqkpe[:], pr[0:64], AF.Square, [prk], ["sqkpe"])
                t1, k1 = ft_new()
                stt(t1[0:64], pr[0:64], gc[0:64, GC_MKR:GC_MKR + 1], ropec[:], ALU.mult, ALU.mult, [prk, "gc", "ropec"], [k1])
                t2, k2 = ft_new()
                stt(t2[0:64], prp[0:64], gc[0:64, GC_MKRP:GC_MKRP + 1], ropes[:], ALU.mult, ALU.mult, [prpk, "gc", "ropes"], [k2])
                tt("dve", kperot[:], t1[0:64], t2[0:64], ALU.add, [k1, k2], ["kperot"])
            add_step(a_kpe, slab_load("kpe", 0, 16 * 128))

            def a_wkvk(slot, sk):
                cells = {}

                def st1(h):
                    pn, pnk = ps_new()
                    gemm_fm(pn, pnk, slot, sk, 2, 1024, h * 128, 128, bg, [("big", 0), ("big", 1)])
                    cells[h] = (pn, pnk) + square_bf(pn, pnk)

                def st2(h):
                    pn, pnk, sq, sqk = cells.pop(h)
                    p2, p2k = ps_new()
                    mm(p2, onesb[:], sq, True, False, ["onesb", sqk], [p2k])
                    mm(p2, onesb[0:64, :], sqkpe[:], False, True, ["onesb", "sqkpe"], [p2k])
                    rs, rk = rstd_from(p2, p2k, 192)
                    kn, knk = bt_new()
                    stt(kn, pn, gc[:, GC_MKN:GC_MKN + 1], rs, ALU.mult, ALU.mult, [pnk, "gc", rk], [knk])
                    P.dma(KTn_d[h, :, k0:k0 + T], kn, reads=[knk])
                    kp, kpk = bt_new()
                    tt("dve", kp[0:64], kperot[:], rs[0:64], ALU.mult, ["kperot", rk], [kpk])
                    P.dma(KTp_d[h, :, k0:k0 + T], kp[0:64], reads=[kpk])
                for h in range(9):
                    if h < 8:
                        st1(h)
                    if h >= 1:
                        st2(h - 1)
            add_step(a_wkvk, slab_load("wkvk", 0, 2 * 1024))

            def a_wkvv(slot, sk):
                for t in range(4):
                    for n in range(2):
                        pp, pk = ps_new()
                        for cc in range(2):
                            mm(pp, bg(cc)[:, t * 128:(t + 1) * 128], slot[:, cc * 1024 + n * 512:cc * 1024 + (n + 1) * 512],
                               cc == 0, cc == 1, [sk, ("big", cc)], [pk])
                        b, bk = bt_new()
                        act(b, pp, AF.Copy, [pk], [bk])
                        P.dma(v_dst(VA_d, n * 4, k0 + t * 128), b.rearrange("p (h d) -> p h d", d=128), reads=[bk])
            add_step(a_wkvv, slab_load("wkvv", 0, 2 * 1024))

            def mk_dkq(name, s, gcol, dst_d, off):
                def fn(slot, sk):
                    cells = {}

                    def st1(i):
                        pp, pk = ps_new()
                        gemm_fm(pp, pk, slot, sk, 16, 512, i * 128, 128, hb, HB_KEYS)
                        cells[i] = (pp, pk) + square_bf(pp, pk)

                    def st2(i):
                        h = s * 4 + i
                        pp, pk, sq, sqk = cells.pop(i)
                        p2, p2k = ps_new()
                        mm(p2, oblk[:], sq, True, True, ["oblk", sqk], [p2k])
                        rs, rk = rstd_from(p2, p2k, 64)
                        kd, kdk = bt_new()
                        stt(kd, pp, gc[:, gcol:gcol + 1], rs, ALU.mult, ALU.mult, [pk, "gc", rk], [kdk])
                        P.dma(dst_d[h, :, off:off + T], kd, reads=[kdk])
                    for i in range(5):
                        if i < 4:
                            st1(i)
                        if i >= 1:
                            st2(i - 1)
                return fn
            add_step(a_prenorm)
            for s in range(2):
                add_step(mk_dkq("dk", s, GC_DK, KDT_d, k0), slab_load("dk", s, 16 * 512))

            def mk_dv(s):
                def fn(slot, sk):
                    for t in range(4):
                        pp, pk = ps_new()
                        for kc in range(16):
                            mm(pp, hb(kc)[:, t * 128:(t + 1) * 128], slot[:, kc * 512:(kc + 1) * 512], kc == 0, kc == 15,
                               [sk, HB_KEYS[kc]], [pk])
                        b, bk = bt_new()
                        act(b, pp, AF.Copy, [pk], [bk])
                        P.dma(v_dst(VD_d, s * 4, k0 + t * 128), b.rearrange("p (h d) -> p h d", d=128), reads=[bk])
                return fn
            for s in range(2):
                add_step(mk_dv(s), slab_load("dv", s, 16 * 512))

            if o0 is None:
                return

            def a_cq(slot, sk):
                pps, sqs = [], []
                for cc in range(4):
                    pp, pk = ps_new()
                    gemm_fm(pp, pk, slot, sk, 16, 512, cc * 128, 128, hb, HB_KEYS)
                    pps.append((pp, pk))
                    sqs.append(square_bf(pp, pk))
                p2, p2k = ps_new()
                for cc in range(4):
                    mm(p2, onesb[:], sqs[cc][0], cc == 0, cc == 3, ["onesb", sqs[cc][1]], [p2k])
                rs, rk = rstd_from(p2, p2k, Q_LORA)
                for cc in range(4):
                    stt(bg(2 + cc), pps[cc][0], gc[:, GC_QA + cc:GC_QA + cc + 1], rs, ALU.mult, ALU.mult,
                        [pps[cc][1], "gc", rk], [("big", 2 + cc)])
            add_step(a_cq, slab_load("cq", 0, 16 * 512))

            CQK = [("big", 2 + cc) for cc in range(4)]

            def a_wqb(slot, sk):
                cqn = lambda cc: bg(2 + cc)
                for h in range(8):
                    pn, pnk = ps_new()
                    gemm_fm(pn, pnk, slot, sk, 4, 2048, h * 256, 128, cqn, CQK)
                    pr, prk = ps_new()
                    gemm_fm(pr, prk, slot, sk, 4, 2048, h * 256 + 128, 64, cqn, CQK)
                    prp, prpk = ps_new()
                    gemm_fm(prp, prpk, slot, sk, 4, 2048, h * 256 + 192, 64, cqn, CQK)
                    sqn, sqnk = square_bf(pn, pnk)
                    sqr, sqrk = square_bf(pr, prk, 64)
                    p2, p2k = ps_new()
                    mm(p2, onesb[:], sqn, True, False, ["onesb", sqnk], [p2k])
                    mm(p2, onesb[0:64, :], sqr[0:64], False, True, ["onesb", sqrk], [p2k])
                    rs, rk = rstd_from(p2, p2k, 192)
                    qn, qnk = bt_new()
                    stt(qn, pn, gc[:, GC_MQN:GC_MQN + 1], rs, ALU.mult, ALU.mult, [pnk, "gc", rk], [qnk])
                    P.dma(QTn_d[h, :, o0:o0 + T], qn, reads=[qnk])
                    t1, k1 = ft_new()
                    stt(t1[0:64], pr[0:64], gc[0:64, GC_MQR:GC_MQR + 1], ropec[:], ALU.mult, ALU.mult, [prk, "gc", "ropec"], [k1])
                    t2, k2 = ft_new()
                    stt(t2[0:64], prp[0:64], gc[0:64, GC_MQRP:GC_MQRP + 1], ropes[:], ALU.mult, ALU.mult, [prpk, "gc", "ropes"], [k2])
                    t3, k3 = ft_new()
                    tt("dve", t3[0:64], t1[0:64], t2[0:64], ALU.add, [k1, k2], [k3])
                    qp, qpk = bt_new()
                    tt("dve", qp[0:64], t3[0:64], rs[0:64], ALU.mult, [k3, rk], [qpk])
                    P.dma(QTp_d[h, :, o0:o0 + T], qp[0:64], reads=[qpk])
            add_step(a_wqb, slab_load("wqb", 0, 4 * 2048))

            for s in range(2):
                add_step(mk_dkq("dq", s, GC_DQ, QDT_d, o0), slab_load("dq", s, 16 * 512))

        for gi in range(len(groupsA)):
            phaseA(gi)
        add_step(lambda slot, sk: P.drain("sp", queues=("sp",)))

        SC_MLA = 192.0 ** -0.5
        SC_DIF = 64.0 ** -0.5

        class Pipe:
            def __init__(self):
                self.q1 = None
                self.q2 = None

            def push(self, item):
                item[0]()
                if self.q1 is not None:
                    self.q1[1]()
                if self.q2 is not None:
                    self.q2[2]()
                self.q2 = self.q1
                self.q1 = item

            def flush(self):
                if self.q1 is not None:
                    self.q1[1]()
                if self.q2 is not None:
                    self.q2[2]()
                if self.q1 is not None:
                    self.q1[2]()
                self.q1 = self.q2 = None

        def own_group(seq, j):
            if seq == "p":
                o0, kt_base, nkb, fcol0 = j * T, 0, nkp, j * nkp
                ysrc = yp_d[j * T:(j + 1) * T, :]
                xk0 = j * T
                nown_kb, ng = NPC // 128, ngp
            else:
                o0, kt_base, nkb, fcol0 = NPC + j * T, S_P, nks, ngp * nkp + j * nks
                ysrc = ys_d[j * T:(j + 1) * T, :]
                xk0 = S_P + j * T
                nown_kb, ng = nks, ngs
            nch = nkb * 128 // CH

            def bias_kind(kb):
                if kb < nown_kb:
                    D = kb - 4 * j
                    if -1 <= D <= 4:
                        return ("tile", 512 - 128 * D)
                elif seq == "p":
                    if j == 0 and kb == nkb - 1:
                        return ("tile", NBW)
                    if j == ng - 1 and kb == nown_kb:
                        return ("tile", NBW + 512)
                return ("far", fcol0 + kb)

            for h in range(8):
                pipe = Pipe()
                for ci in range(nch):
                    kt0 = kt_base + ci * CH
                    chg = kt0 // CH
                    loads = [
                        (lambda slot: slot[:, 0:CH], KTn_d[h, :, kt0:kt0 + CH], []),
                        (lambda slot: slot[0:64, CH:2 * CH], KTp_d[h, :, kt0:kt0 + CH], []),
                        (lambda slot: slot[:, 2 * CH:3 * CH], VA_d[h, chg].rearrange("p k d -> p (k d)"), []),
                    ]

                    def fn(slot, sk, h=h, ci=ci, pipe=pipe):
                        b = h % 2
                        if h == 0 and ci == 0:
                            state["psl"] = [0, 1, 2, 5, 6, 7]
                            load_x(xk0)
                            P.dma(qbuf[:, 0, 0:T], QTn_d[0, :, o0:o0 + T], writes=[("q", 0)])
                            P.dma(qbuf[0:64, 0, T:2 * T], QTp_d[0, :, o0:o0 + T], writes=[("q", 0)])
                        if ci == 0 and h + 1 < 8:
                            P.dma(qbuf[:, 1 - b, 0:T], QTn_d[h + 1, :, o0:o0 + T], writes=[("q", 1 - b)])
                            P.dma(qbuf[0:64, 1 - b, T:2 * T], QTp_d[h + 1, :, o0:o0 + T], writes=[("q", 1 - b)])
                        qn = qbuf[:, b, 0:T]
                        qp = qbuf[0:64, b, T:2 * T]
                        for kbi in range(NKB_CH):
                            kb = ci * NKB_CH + kbi
                            first, last = (kb == 0), (kb == nkb - 1)
                            cell = {}

                            def s_fn(cell=cell, kbi=kbi):
                                pS, pSk = ps_new()
                                cell["pS"] = (pS, pSk)
                                mm(pS, slot[:, kbi * 128:(kbi + 1) * 128], qn, True, False, [sk, ("q", b)], [pSk])
                                mm(pS, slot[0:64, CH + kbi * 128:CH + (kbi + 1) * 128], qp, False, True, [sk, ("q", b)], [pSk])

                            def e_fn(cell=cell):
                                pS, pSk = cell["pS"]
                                pt_, ptk = bt_new()
                                cell["pt"] = (pt_, ptk)
                                act(pt_, pS, AF.Exp, [pSk], [ptk], scale=SC_MLA)

                            def p_fn(cell=cell, kbi=kbi, first=first, last=last):
                                pt_, ptk = cell["pt"]
                                mm(psa[3], slot[:, 2 * CH + kbi * 128:2 * CH + (kbi + 1) * 128], pt_, first, last,
                                   [sk, ptk], [("ps", 3)])
                                mm(psa[4], onesb[:], pt_, first, last, ["onesb", ptk], [("ps", 4)])
                            pipe.push((s_fn, e_fn, p_fn))
                        if ci == nch - 1:
                            pipe.flush()
                            rc, rck = ft_new()
                            P.op("dve", lambda e, rc=rc: e.reciprocal(out=rc, in_=psa[4]), reads=[("ps", 4)], writes=[rck])
                            tt("dve", bg(16 + h), psa[3], rc, ALU.mult, [("ps", 3), rck], [("big", 16 + h)])
                    add_step(fn, loads, att=True)

            for h in range(8):
                pipe = Pipe()
                for ci in range(nch):
                    kt0 = kt_base + ci * CH
                    chg = kt0 // CH
                    loads = [
                        (lambda slot: slot[:, 0:CH], KDT_d[h, :, kt0:kt0 + CH], []),
                        (lambda slot: slot[:, CH:2 * CH], VD_d[h, chg].rearrange("p k d -> p (k d)"), []),
                    ]

                    def fn(slot, sk, h=h, ci=ci, pipe=pipe):
                        b = h % 2
                        if ci == 0:
                            if h == 0:
                                state["psl"] = [0, 1, 2, 7]
                                P.dma(qbuf[:, 0, 0:T], QDT_d[0, :, o0:o0 + T], writes=[("q", 0)])
                            if h + 1 < 8:
                                P.dma(qbuf[:, 1 - b, 0:T], QDT_d[h + 1, :, o0:o0 + T], writes=[("q", 1 - b)])
                            P.dma(biasb[:], biasD_d[h], writes=["biasb"])
                            P.dma(farb[:], farbD_d[h], writes=["farb"])
                        qd = qbuf[:, b, 0:T]
                        for kbi in range(NKB_CH):
                            kb = ci * NKB_CH + kbi
                            first, last = (kb == 0), (kb == nkb - 1)
                            kind = bias_kind(kb)
                            cell = {}

                            def s_fn(cell=cell, kbi=kbi):
                                for m in range(2):
                                    pS, pSk = ps_new()
                                    cell["pS", m] = (pS, pSk)
                                    mm(pS, slot[m * 64:(m + 1) * 64, kbi * 128:(kbi + 1) * 128], qd[m * 64:(m + 1) * 64, :],
                                       True, True, [sk, ("q", b)], [pSk])

                            def e_fn(cell=cell, kind=kind):
                                for m in range(2):
                                    pS, pSk = cell["pS", m]
                                    pt_, ptk = bt_new()
                                    cell["pt", m] = (pt_, ptk)
                                    if kind[0] == "far":
                                        act(pt_, pS, AF.Exp, [pSk, "farb"], [ptk], scale=SC_DIF,
                                            bias=farb[:, kind[1]:kind[1] + 1])
                                    else:
                                        tmp, tk = ft_new()
                                        stt(tmp, pS, SC_DIF, biasb[:, kind[1]:kind[1] + T], ALU.mult, ALU.add,
                                            [pSk, "biasb"], [tk])
                                        act(pt_, tmp, AF.Exp, [tk], [ptk])

                            def p_fn(cell=cell, kbi=kbi, first=first, last=last):
                                for m in range(2):
                                    pt_, ptk = cell["pt", m]
                                    mm(psa[3 + 2 * m], slot[:, CH + kbi * 128:CH + (kbi + 1) * 128], pt_, first, last,
                                       [sk, ptk], [("ps", 3 + 2 * m)])
                                    mm(psa[4 + 2 * m], onesb[:], pt_, first, last, ["onesb", ptk], [("ps", 4 + 2 * m)])
                            pipe.push((s_fn, e_fn, p_fn))
                        if ci == nch - 1:
                            pipe.flush()
                            rc0, k0_ = ft_new()
                            P.op("dve", lambda e, rc0=rc0: e.reciprocal(out=rc0, in_=psa[4]), reads=[("ps", 4)], writes=[k0_])
                            t0, tk0 = ft_new()
                            tt("dve", t0, psa[3], rc0, ALU.mult, [("ps", 3), k0_], [tk0])
                            rc1, k1_ = ft_new()
                            P.op("dve", lambda e, rc1=rc1: e.reciprocal(out=rc1, in_=psa[6]), reads=[("ps", 6)], writes=[k1_])
                            tsc("dve", rc1, rc1, small[:, 2:3], None, ALU.mult, None, [k1_, "neglam"], [k1_])
                            t1, tk1 = ft_new()
                            tt("dve", t1, psa[5], rc1, ALU.mult, [("ps", 5), k1_], [tk1])
                            ob, obk = ft_new()
                            tt("dve", ob, t0, t1, ALU.add, [tk0, tk1], [obk])
                            sq, sqk = bt_new()
                            tt("dve", sq, ob, ob, ALU.mult, [obk], [sqk])
                            p2, p2k = ps_new()
                            mm(p2, onesb[:], sq, True, True, ["onesb", sqk], [p2k])
                            rs, rk = rstd_from(p2, p2k, 128)
                            stt(bg(24 + h), ob, gc[:, GC_SUB:GC_SUB + 1], rs, ALU.mult, ALU.mult, [obk, "gc", rk], [("big", 24 + h)])
                            if h == 7:
                                state["psl"] = list(range(7))
                    add_step(fn, loads, att=True)

            def c_norm(slot, sk):
                norm_T(GC_MIX)
                transposes(GC_MIX)
            add_step(c_norm)

            for s in range(4):
                cellg = {}

                def c_ga(slot, sk, s=s, cellg=cellg):
                    for c in range(4):
                        pp, pk = ps_new()
                        gemm_fm(pp, pk, slot, sk, 16, 512, c * 128, 128, hb, HB_KEYS)
                        f, fk = ft_new()
                        act(f, pp, AF.Sigmoid, [pk], [fk])
                        cellg["a", c] = (f, fk)
                add_step(c_ga, slab_load("ga", s, 16 * 512))

                def c_upa(slot, sk, s=s, cellg=cellg):
                    for c in range(4):
                        pp, pk = ps_new()
                        gemm_fm(pp, pk, slot, sk, 8, 512, c * 128, 128, lambda kc: bg(16 + kc), [("big", 16 + i) for i in range(8)])
                        f, fk = cellg["a", c]
                        tt("dve", f, pp, f, ALU.mult, [pk, fk], [fk])
                add_step(c_upa, slab_load("upa", s, 8 * 512))

                def c_gb(slot, sk, s=s, cellg=cellg):
                    for c in range(4):
                        pp, pk = ps_new()
                        gemm_fm(pp, pk, slot, sk, 16, 512, c * 128, 128, hb, HB_KEYS)
                        f, fk = ft_new()
                        act(f, pp, AF.Sigmoid, [pk], [fk])
                        cellg["b", c] = (f, fk)
                add_step(c_gb, slab_load("gb", s, 16 * 512))

                def c_upd(slot, sk, s=s, cellg=cellg):
                    for c in range(4):
                        pp, pk = ps_new()
                        gemm_fm(pp, pk, slot, sk, 8, 512, c * 128, 128, lambda kc: bg(24 + kc), [("big", 24 + i) for i in range(8)])
                        f, fk = cellg["b", c]
                        tt("dve", f, pp, f, ALU.mult, [pk, fk], [fk])
                        fa, fak = cellg["a", c]
                        tt("dve", bg(4 * s + c), fa, f, ALU.add, [fak, fk], [("big", 4 * s + c)])
                add_step(c_upd, slab_load("upd", s, 8 * 512))

            MK = [("big", i) for i in range(16)]
            for n in range(4):
                def c_wo(slot, sk, n=n):
                    for t in range(4):
                        pp, pk = ps_new()
                        for kc in range(16):
                            mm(pp, bg(kc)[:, t * 128:(t + 1) * 128], slot[:, kc * 512:(kc + 1) * 512], kc == 0, kc == 15,
                               [sk, MK[kc]], [pk])
                        xs_ = xb(t)[:, n * 512:(n + 1) * 512]
                        tt("dve", xs_, pp, xs_, ALU.add, [pk, ("x", t)], [("x", t)])
                add_step(c_wo, slab_load("wo", n, 16 * 512))

            def f_norm(slot, sk):
                norm_T(GC_FFN)
                transposes(GC_FFN)
            add_step(f_norm)

            for s in range(11):
                cellf = {}

                def f_g(slot, sk, s=s, cellf=cellf):
                    for c in range(4):
                        pp, pk = ps_new()
                        gemm_fm(pp, pk, slot, sk, 16, 512, c * 128, 128, hb, HB_KEYS)
                        f, fk = ft_new()
                        act(f, pp, AF.Silu, [pk], [fk])
                        cellf[c] = (f, fk)
                add_step(f_g, slab_load("wg", s, 16 * 512))

                def f_u(slot, sk, s=s, cellf=cellf):
                    for c in range(4):
                        pp, pk = ps_new()
                        gemm_fm(pp, pk, slot, sk, 16, 512, c * 128, 128, hb, HB_KEYS)
                        f, fk = cellf[c]
                        tt("dve", bg(4 * s + c), pp, f, ALU.mult, [pk, fk], [("big", 4 * s + c)])
                add_step(f_u, slab_load("wu", s, 16 * 512))

            for n in range(4):
                for part in range(4):
                    def f_d(slot, sk, n=n, part=part):
                        for t in range(4):
                            for kc in range(11):
                                ch = part * 11 + kc
                                mm(psa[t], bg(ch)[:, t * 128:(t + 1) * 128], slot[:, kc * 512:(kc + 1) * 512],
                                   part == 0 and kc == 0, part == 3 and kc == 10, [sk, ("big", ch)], [("ps", t)])
                        if part == 3:
                            for t in range(4):
                                xs_ = xb(t)[:, n * 512:(n + 1) * 512]
                                tt("dve", xs_, psa[t], xs_, ALU.add, [("ps", t), ("x", t)], [("x", t)])
                            if n == 3:
                                for t in range(4):
                                    P.dma(ysrc[t * 128:(t + 1) * 128, :], xb(t), reads=[("x", t)])
                    add_step(f_d, slab_load("wd", n * 4 + part, 11 * 512))

        if stage >= 4:
            for j in range(ngp):
                own_group("p", j)
        if stage >= 4:
            for j in range(ngs):
                own_group("s", j)

        load_steps = [i for i, (fn, l, a) in enumerate(steps) if l]
        issued = [0]

        def issue_upto(m):
            while issued[0] <= m and issued[0] < len(load_steps):
                idx = load_steps[issued[0]]
                si = issued[0] % 3
                for (dst_fn, src, rk) in steps[idx][1]:
                    P.dma(dst_fn(wslot[si]), src, reads=rk, writes=[("wslot", si)])
                issued[0] += 1

        nexec = 0
        for i, (fn, loads, att) in enumerate(steps):
            if max_steps is not None and i >= max_steps:
                break
            if loads:
                prev_att = nexec > 0 and steps[load_steps[nexec - 1]][2]
                issue_upto(nexec + (1 if prev_att else 2))
                si = nexec % 3
                fn(wslot[si], ("wslot", si))
                nexec += 1
            else:
                fn(None, None)

    return nc, P, st, body


_CACHE = {}


def run(inputs, S_P, S_S, debug=False, stage=9, max_steps=None, ncores=NCORES, first=0):
    key = (S_P, S_S, debug, stage, max_steps)
    if key not in _CACHE:
        _CACHE[key] = build_program(S_P, S_S, debug, stage, max_steps)
    nc = _CACHE[key]
    maps = host_prep(inputs, S_P, S_S)
    res = run_bass_kernel_spmd(nc, maps[first:first + ncores], core_ids=list(range(ncores)))
    return res


SINGLE_LAUNCH = True


def kernel(**inputs):
    S_P = int(np.asarray(inputs["x_prompt"]).shape[1])
    S_S = int(np.asarray(inputs["x_sample"]).shape[1])
    NPC = S_P // NCORES
    yp = np.empty((1, S_P, D_MODEL), np.float32)
    ys = np.empty((NCORES, S_S, D_MODEL), np.float32)
    if SINGLE_LAUNCH:
        res = run(inputs, S_P, S_S)
        for c in range(NCORES):
            yp[0, c * NPC:(c + 1) * NPC] = np.asarray(res.results[c]["yp"], dtype=np.float32)
            ys[c] = np.asarray(res.results[c]["ys"], dtype=np.float32)
        return (yp, ys)
    key = (S_P, S_S, False, 9, None)
    if key not in _CACHE:
        _CACHE[key] = build_program(S_P, S_S)
    nc = _CACHE[key]
    maps = host_prep(inputs, S_P, S_S)
    for c in range(NCORES):
        res = run_bass_kernel_spmd(nc, [maps[c]], core_ids=[0])
        yp[0, c * NPC:(c + 1) * NPC] = np.asarray(res.results[0]["yp"], dtype=np.float32)
        ys[c] = np.asarray(res.results[0]["ys"], dtype=np.float32)
        maps[c] = None
    return (yp, ys)
```

```python
import bisect
import contextlib
import math
import numpy as np
import concourse.bass as bass
import concourse.mybir as mybir
from concourse.bass_utils import run_bass_kernel_spmd

F32 = mybir.dt.float32
BF16 = mybir.dt.bfloat16
ALU = mybir.AluOpType
AF = mybir.ActivationFunctionType
AX = mybir.AxisListType

D_MODEL = 2048
MLA_HEADS = 8
Q_LORA = 512
KV_LORA = 256
D_FF = 5632
EPS = 1e-6
LAM_INIT = 0.8 - 0.6 * math.exp(0.0)
NCORES = 8
CAST_BARRIER = True
T = 512
NBW = 1152
LW = 1279


class Prog:
    COMPUTE = ("pe", "act", "dve", "pool")

    def __init__(self, nc, nsp=8, npool=4):
        self.nc = nc
        self.q = {e: [] for e in ("pe", "act", "dve", "pool", "sp")}
        self.sems = {}
        self.sem_ctx = []
        for e in self.COMPUTE:
            self.sems[e] = self._sem("c_" + e)
        self.dq = {"sp": [self._sem(f"dsp{i}") for i in range(nsp)],
                   "pool": [self._sem(f"dpl{i}") for i in range(npool)]}
        self.dn = {"sp": 0, "pool": 0}
        self.nops = {e: 0 for e in self.COMPUTE}
        self.inc_idx = {e: [] for e in self.COMPUTE}
        self.ents = {e: {} for e in self.COMPUTE}
        self.seen = {e: {} for e in self.q}
        self.last_w = {}
        self.readers = {}

    def _sem(self, name):
        ctx = self.nc.semaphore(name)
        s = ctx.__enter__()
        self.sem_ctx.append(ctx)
        return s

    def _need(self, eng, tok, waits):
        if tok is None:
            return
        if tok[0] == "c":
            _, e, idx = tok
            if e == eng and e == "pe":
                return
            lst = self.inc_idx[e]
            p = bisect.bisect_left(lst, idx)
            if p < len(lst):
                val = p + 1
            else:
                self.ents[e][idx]["inc"] = True
                lst.append(idx)
                val = len(lst)
            sem = self.sems[e]
        else:
            _, sem, val = tok
        sid = id(sem)
        if self.seen[eng].get(sid, 0) >= val:
            return
        self.seen[eng][sid] = val
        waits.append((sem, val))

    @staticmethod
    def _chan(tok):
        return tok[1] if tok[0] == "c" else id(tok[1])

    def _deps(self, eng, reads, writes, is_dma):
        need = {}

        def add(t):
            if t is None:
                return
            if (not is_dma) and t[0] == "c" and t[1] == eng and eng == "pe":
                return
            c = self._chan(t)
            o = need.get(c)
            if o is None or t[2] > o[2]:
                need[c] = t

        for k in reads:
            add(self.last_w.get(k))
        for k in writes:
            t = self.last_w.get(k)
            if t is not None and (is_dma or not (t[0] == "c" and t[1] == eng)):
                add(t)
            for r in self.readers.get(k, {}).values():
                if is_dma or not (r[0] == "c" and r[1] == eng):
                    add(r)
        waits = []
        for t in need.values():
            self._need(eng, t, waits)
        return waits

    def _commit(self, tok, reads, writes):
        c = self._chan(tok)
        for k in reads:
            d = self.readers.setdefault(k, {})
            o = d.get(c)
            if o is None or tok[2] > o[2]:
                d[c] = tok
        for k in writes:
            self.last_w[k] = tok
            self.readers[k] = {}

    def op(self, eng, fn, reads=(), writes=()):
        waits = self._deps(eng, reads, writes, False)
        self.nops[eng] += 1
        idx = self.nops[eng]
        ent = {"fn": fn, "waits": waits, "inc": False}
        self.ents[eng][idx] = ent
        self.q[eng].append(ent)
        tok = ("c", eng, idx)
        self._commit(tok, reads, writes)
        return tok

    def dma(self, out, in_, reads=(), writes=(), queue="sp"):
        waits = self._deps(queue, reads, writes, True)
        pool = self.dq[queue]
        n = self.dn[queue]
        self.dn[queue] += 1
        sem = pool[n % len(pool)]
        rnd = n // len(pool)
        if rnd > 0:
            self._need(queue, ("d", sem, 16 * rnd), waits)
        tok = ("d", sem, 16 * (rnd + 1))
        ent = {"fn": (lambda e, o=out, i=in_: e.dma_start(out=o, in_=i)),
               "waits": waits, "dma": sem}
        self.q[queue].append(ent)
        self._commit(tok, reads, writes)
        return tok

    def drain(self, eng, queues=("sp", "pool")):
        waits = []
        for qn in queues:
            pool = self.dq[qn]
            n = self.dn[qn]
            for i, sem in enumerate(pool):
                cnt = (n - i + len(pool) - 1) // len(pool) if n > i else 0
                if cnt > 0:
                    self._need(eng, ("d", sem, 16 * cnt), waits)
        if waits:
            self.q[eng].append({"fn": None, "waits": waits})

    def emit(self):
        nc = self.nc
        handles = {"pe": "tensor", "act": "scalar", "dve": "vector", "pool": "gpsimd", "sp": "sync"}
        with nc.Block() as block:
            for ename, attr in handles.items():
                ents = self.q[ename]
                if not ents:
                    continue
                csem = self.sems.get(ename)

                def body(eng, ents=ents, csem=csem):
                    for ent in ents:
                        for (s, v) in ent["waits"]:
                            eng.wait_ge(s, v)
                        if ent["fn"] is None:
                            continue
                        ins = ent["fn"](eng)
                        if "dma" in ent:
                            ins.then_inc(ent["dma"], 16)
                        elif ent["inc"]:
                            ins.then_inc(csem, 1)
                getattr(block, attr)(body)
        for ctx in reversed(self.sem_ctx):
            ctx.__exit__(None, None, None)

def t5_bucket_np(rel):
    nb = 16
    max_exact = 8
    rel = np.asarray(rel, dtype=np.int64)
    ret = np.where(rel > 0, nb, 0)
    n = np.abs(rel)
    nf = np.maximum(n, 1).astype(np.float32)
    ratio = np.log(nf / np.float32(max_exact)) / np.float32(math.log(128 / max_exact))
    large = max_exact + (ratio.astype(np.float32) * np.float32(nb - max_exact)).astype(np.int32)
    large = np.minimum(large, nb - 1)
    return (ret + np.where(n < max_exact, n, large)).astype(np.int64)


def onehot32(b):
    return (np.arange(32)[:, None] == np.asarray(b).reshape(1, -1)).astype(np.float32)


def rope_tables(pos):
    half = 32
    inv = (10000.0 ** (-np.arange(half, dtype=np.float32) / half)).astype(np.float32)
    ang = pos.astype(np.float32)[None, :] * inv[:, None]
    cos = np.cos(ang).astype(np.float32)
    sin = np.sin(ang).astype(np.float32)
    c = np.concatenate([cos, cos], axis=0)
    s = np.concatenate([-sin, sin], axis=0)
    return np.ascontiguousarray(c), np.ascontiguousarray(s)


def swap_halves(a, axis=-1):
    h = a.shape[axis] // 2
    lo = np.take(a, np.arange(0, h), axis=axis)
    hi = np.take(a, np.arange(h, 2 * h), axis=axis)
    return np.concatenate([hi, lo], axis=axis)


def host_prep(inp, S_P, S_S):
    NPC = S_P // NCORES
    f = lambda a: np.ascontiguousarray(np.asarray(a, dtype=np.float32))
    xP = f(inp["x_prompt"])[0]
    xS = f(inp["x_sample"])
    w_in = f(inp["w_in"])[0]
    wq_b = f(inp["wq_b"])[0]
    shared = {
        "w_in": w_in,
        "w_kpeP": np.ascontiguousarray(swap_halves(w_in[:, 768:832])),
        "wq_b": wq_b,
        "wq_ropeP": np.ascontiguousarray(
            swap_halves(wq_b.reshape(512, 8, 192)[:, :, 128:192]).reshape(512, 512)),
        "wkv_b": f(inp["wkv_b"])[0],
        "w_up_mla": f(inp["w_up_mla"])[0],
        "w_up_diff": f(inp["w_up_diff"])[0],
        "w_o": f(inp["w_o"])[0],
        "w_gate": f(inp["w_gate"])[0],
        "w_up": f(inp["w_up"])[0],
        "w_down": f(inp["w_down"])[0],
        "relb": f(inp["rel_bias"]),
        "ident": np.eye(128, dtype=np.float32),
    }
    gc = np.zeros((128, 48), np.float32)
    gc[:, 0:16] = f(inp["mix_norm"])[0].reshape(16, 128).T
    gc[:, 16:32] = f(inp["ffn_norm"])[0].reshape(16, 128).T
    gc[:, 32:36] = f(inp["q_a_norm"])[0].reshape(4, 128).T
    gc[:, 36:38] = f(inp["kv_a_norm"])[0].reshape(2, 128).T
    mq = f(inp["mla_q_norm"])[0]
    mk = f(inp["mla_k_norm"])[0]
    gc[:, 38] = mq[0:128]
    gc[0:64, 39] = mq[128:192]
    gc[0:64, 40] = swap_halves(mq[128:192])
    gc[:, 41] = mk[0:128]
    gc[0:64, 42] = mk[128:192]
    gc[0:64, 43] = swap_halves(mk[128:192])
    gc[:, 44] = np.tile(f(inp["diff_q_norm"])[0], 2)
    gc[:, 45] = np.tile(f(inp["diff_k_norm"])[0], 2)
    gc[:, 46] = f(inp["diff_subln"])[0]
    shared["gcols"] = gc
    lam = np.stack([f(inp["lambda_q1"])[0], f(inp["lambda_k1"])[0],
                    f(inp["lambda_q2"])[0], f(inp["lambda_k2"])[0]], 0).reshape(1, 256)
    shared["lamv"] = np.ascontiguousarray(np.tile(lam, (128, 1)))
    shared["ohw"] = onehot32(t5_bucket_np(639 - np.arange(LW)))

    ngp, nkp = NPC // T, S_P // 128
    ngs, nks = S_S // T, S_S // 128
    maps = []
    for c in range(NCORES):
        m = dict(shared)
        m["xp"] = np.ascontiguousarray(np.roll(xP, -c * NPC, axis=0))
        m["xs"] = np.ascontiguousarray(xS[c])
        posP = (np.arange(S_P) + c * NPC) % S_P
        posS = np.arange(S_S)
        cp, sp = rope_tables(posP)
        cs, ss = rope_tables(posS)
        m["ropeC"] = np.ascontiguousarray(np.concatenate([cp, cs], 1))
        m["ropeS"] = np.ascontiguousarray(np.concatenate([sp, ss], 1))
        cols = []
        for j in range(ngp):
            qlo, qhi = posP[j * T], posP[j * T + T - 1]
            for kb in range(nkp):
                klo = posP[kb * 128]
                rel = (klo - qhi) if klo > qhi else (klo + 127 - qlo)
                cols.append(int(t5_bucket_np(rel)))
        for j in range(ngs):
            for kb in range(nks):
                klo = kb * 128
                rel = (klo - (j * T + T - 1)) if klo > j * T + T - 1 else (klo + 127 - j * T)
                cols.append(int(t5_bucket_np(rel)))
        cols += [15, 31]
        m["farsel"] = onehot32(np.array(cols))
        selL = 1.0 if c > 0 else 0.0
        selR = 1.0 if c < NCORES - 1 else 0.0
        m["sel"] = np.ascontiguousarray(
            np.tile(np.array([[selL, 1 - selL, selR, 1 - selR]], np.float32), (128, 1)))
        maps.append(m)
    return maps

class _Stop(Exception):
    pass


def build_program(S_P, S_S, debug=False, stage=9, max_steps=None):
    nc, P, st, body = _build_body(S_P, S_S, debug, stage, max_steps)
    try:
        body()
    except _Stop:
        pass
    P.drain("sp", queues=("sp", "pool"))
    P.emit()
    st.close()
    return nc


def _build_body(S_P, S_S, debug, stage, max_steps=None):
    NPC = S_P // NCORES
    NKEY = S_P + S_S
    NOWN = NPC + S_S
    CH = min(2048, S_S)
    NKB_CH = CH // 128
    ngp, nkp = NPC // T, S_P // 128
    ngs, nks = S_S // T, S_S // 128
    NF = ngp * nkp + ngs * nks + 2
    assert NF <= 1024 and NPC % T == 0 and S_S % T == 0 and S_P % CH == 0

    nc = bass.Bass("TRN2", target_bir_lowering=False)
    P = Prog(nc)

    def din(name, shape):
        return nc.dram_tensor(name, list(shape), F32, kind="ExternalInput")

    def dscr(name, shape, dt=BF16):
        return nc.dram_tensor(name, list(shape), dt, kind="Internal")

    xp_d = din("xp", [S_P, D_MODEL]).ap()
    xs_d = din("xs", [S_S, D_MODEL]).ap()
    w_in_d = din("w_in", [2048, 8000]).ap()
    w_kpeP_d = din("w_kpeP", [2048, 64]).ap()
    wq_b_d = din("wq_b", [512, 1536]).ap()
    wq_ropeP_d = din("wq_ropeP", [512, 512]).ap()
    wkv_b_d = din("wkv_b", [256, 2048]).ap()
    w_up_mla_d = din("w_up_mla", [1024, 2048]).ap()
    w_up_diff_d = din("w_up_diff", [1024, 2048]).ap()
    w_o_d = din("w_o", [2048, 2048]).ap()
    w_gate_d = din("w_gate", [2048, D_FF]).ap()
    w_up_d = din("w_up", [2048, D_FF]).ap()
    w_down_d = din("w_down", [D_FF, 2048]).ap()
    relb_d = din("relb", [32, 8]).ap()
    ident_d = din("ident", [128, 128]).ap()
    gcols_d = din("gcols", [128, 48]).ap()
    lamv_d = din("lamv", [128, 256]).ap()
    ohw_d = din("ohw", [32, LW]).ap()
    ropeC_d = din("ropeC", [64, NKEY]).ap()
    ropeS_d = din("ropeS", [64, NKEY]).ap()
    farsel_d = din("farsel", [32, NF]).ap()
    sel_d = din("sel", [128, 4]).ap()
    yp_d = nc.dram_tensor("yp", [NPC, D_MODEL], F32, kind="ExternalOutput").ap()
    ys_d = nc.dram_tensor("ys", [S_S, D_MODEL], F32, kind="ExternalOutput").ap()

    slabs = {}

    def mkslab(name, n, kc, w):
        slabs[name] = dscr("s_" + name, [n, 128, kc, w]).ap()

    mkslab("cq", 1, 16, 512); mkslab("ckv", 1, 16, 256); mkslab("kpe", 1, 16, 128)
    mkslab("dq", 2, 16, 512); mkslab("dk", 2, 16, 512); mkslab("dv", 2, 16, 512)
    mkslab("ga", 4, 16, 512); mkslab("gb", 4, 16, 512)
    mkslab("wqb", 1, 4, 2048); mkslab("wkvk", 1, 2, 1024); mkslab("wkvv", 1, 2, 1024)
    mkslab("upa", 4, 8, 512); mkslab("upd", 4, 8, 512); mkslab("wo", 4, 16, 512)
    mkslab("wg", 11, 16, 512); mkslab("wu", 11, 16, 512); mkslab("wd", 16, 11, 512)

    kind_dbg = "ExternalOutput" if debug else "Internal"

    def dscr2(name, shape, dt=BF16):
        return nc.dram_tensor(name, list(shape), dt, kind=kind_dbg)

    KTn_d = dscr2("KTn", [8, 128, NKEY]).ap()
    KTp_d = dscr2("KTp", [8, 64, NKEY]).ap()
    KDT_d = dscr2("KDT", [8, 128, NKEY]).ap()
    VA_d = dscr2("VA", [8, NKEY // CH, 128, NKB_CH, 128]).ap()
    VD_d = dscr2("VD", [8, NKEY // CH, 128, NKB_CH, 128]).ap()
    QTn_d = dscr2("QTn", [8, 128, NOWN]).ap()
    QTp_d = dscr2("QTp", [8, 64, NOWN]).ap()
    QDT_d = dscr2("QDT", [8, 128, NOWN]).ap()
    rrep_t = dscr("rrep", [8, 128, LW], F32)
    rrep_d = rrep_t.ap()
    biasD_d = dscr2("biasD", [8, 128, NBW + 1024], F32).ap()
    farbD_d = dscr2("farbD", [8, 128, NF], F32).ap()

    st = contextlib.ExitStack()

    def sb(name, shape, dt):
        return st.enter_context(nc.sbuf_tensor(name, list(shape), dt))

    def pst(name, shape, dt):
        return st.enter_context(nc.psum_tensor(name, list(shape), dt))

    xbuf = sb("xbuf", [128, 4 * 2048], F32)
    hbuf = sb("hbuf", [128, 16 * T], BF16)
    big = sb("big", [128, 48 * T], BF16)
    wslot = [sb(f"wslot{i}", [128, 8192], BF16) for i in range(3)]
    ftmp = sb("ftmp", [128, 8, T], F32)
    btmp = sb("btmp", [128, 6, T], BF16)
    qbuf = sb("qbuf", [128, 2, 1024], BF16)
    biasb = sb("biasb", [128, NBW + 1024], F32)
    farb = sb("farb", [128, NF], F32)
    gc = sb("gc", [128, 48], F32)
    identb = sb("identb", [128, 128], BF16)
    onesb = sb("onesb", [128, 128], BF16)
    oblk = sb("oblk", [128, 128], BF16)
    ones32 = sb("ones32", [32, 128], F32)
    relbT = sb("relbT", [32, 8], F32)
    selt = sb("selt", [128, 4], F32)
    lamt = sb("lamt", [128, 256], F32)
    small = sb("small", [128, 32], F32)
    ropec = sb("ropec", [64, T], F32)
    ropes = sb("ropes", [64, T], F32)
    kperot = sb("kperot", [64, T], F32)
    stat = sb("stat", [128, 16], F32)
    btk = sb("btk", [128, 4, T], BF16)
    sqkpe = sb("sqkpe", [64, T], BF16)
    psb = [pst(f"ps{i}", [128, T], F32) for i in range(8)]
    psa = [p[:] for p in psb]
    ptr = psa[7].bitcast(BF16)
    assert tuple(ptr.shape) == (128, 2 * T), ptr.shape

    state = {"ps": 0, "psl": list(range(7)), "ft": 0, "bt": 0, "pt": 0}

    def ps_new():
        l = state["psl"]
        i = l[state["ps"] % len(l)]
        state["ps"] += 1
        return psa[i], ("ps", i)

    def ft_new():
        i = state["ft"] % 8
        state["ft"] += 1
        return ftmp[:, i, :], ("ft", i)

    def bt_new():
        i = state["bt"] % 6
        state["bt"] += 1
        return btmp[:, i, :], ("bt", i)

    def pt_new():
        i = state["pt"] % 2
        state["pt"] += 1
        return ptr[:, i * T:(i + 1) * T], ("ps", 7)

    def mm(out, lhsT, rhs, start, stop, reads, writes):
        P.op("pe", lambda e: e.matmul(out, lhsT=lhsT, rhs=rhs, start=start, stop=stop),
             reads=reads, writes=writes)

    def act(out, in_, func, reads, writes, **kw):
        P.op("act", lambda e: e.activation(out=out, in_=in_, func=func, **kw),
             reads=reads, writes=writes)

    def tsc(eng, out, in0, s1, s2, op0, op1, reads, writes):
        if s2 is None:
            P.op(eng, lambda e: e.tensor_scalar(out=out, in0=in0, scalar1=s1, scalar2=None, op0=op0),
                 reads=reads, writes=writes)
        else:
            P.op(eng, lambda e: e.tensor_scalar(out=out, in0=in0, scalar1=s1, scalar2=s2,
                                                op0=op0, op1=op1), reads=reads, writes=writes)

    def tt(eng, out, in0, in1, op, reads, writes):
        P.op(eng, lambda e: e.tensor_tensor(out=out, in0=in0, in1=in1, op=op),
             reads=reads, writes=writes)

    def stt(out, in0, s, in1, op0, op1, reads, writes):
        P.op("dve", lambda e: e.scalar_tensor_tensor(out=out, in0=in0, scalar=s, in1=in1,
                                                      op0=op0, op1=op1), reads=reads, writes=writes)

    def rstd_from(ps_ap, ps_key, dim, npart=128):
        t1, k1 = ft_new()
        act(t1[:npart], ps_ap[:npart], AF.Sqrt, [ps_key], [k1], scale=1.0 / dim, bias=EPS)
        t2, k2 = ft_new()
        P.op("dve", lambda e: e.reciprocal(out=t2[:npart], in_=t1[:npart]), reads=[k1], writes=[k2])
        return t2, k2

    GC_MIX, GC_FFN, GC_QA, GC_KVA = 0, 16, 32, 36
    GC_MQN, GC_MQR, GC_MQRP, GC_MKN, GC_MKR, GC_MKRP, GC_DQ, GC_DK, GC_SUB = 38, 39, 40, 41, 42, 43, 44, 45, 47

    def body():
        P.dma(gc[:], gcols_d, writes=["gc"])
        P.dma(ftmp[:, 0, 0:128], ident_d, writes=[("ft", 0)])
        P.dma(lamt[:], lamv_d, writes=["lamt"])
        P.dma(relbT[:], relb_d, writes=["relbT"])
        P.dma(selt[:], sel_d, writes=["selt"])
        P.op("dve", lambda e: e.tensor_copy(out=identb[:], in_=ftmp[:, 0, 0:128]), reads=[("ft", 0)], writes=["identb"])
        P.op("dve", lambda e: e.memset(onesb[:], 1.0), writes=["onesb"])
        P.op("dve", lambda e: e.memset(oblk[:], 0.0), writes=["oblk"])
        P.op("dve", lambda e: e.memset(oblk[0:64, 0:64], 1.0), writes=["oblk"])
        P.op("dve", lambda e: e.memset(oblk[64:128, 64:128], 1.0), writes=["oblk"])
        P.op("dve", lambda e: e.memset(ones32[:], 1.0), writes=["ones32"])
        tt("dve", ftmp[:, 1, 0:64], lamt[:, 0:64], lamt[:, 64:128], ALU.mult, ["lamt"], [("ft", 1)])
        tt("dve", ftmp[:, 1, 64:128], lamt[:, 128:192], lamt[:, 192:256], ALU.mult, ["lamt"], [("ft", 1)])
        P.op("dve", lambda e: e.reduce_sum(out=small[:, 0:1], in_=ftmp[:, 1, 0:64], axis=AX.X), reads=[("ft", 1)], writes=["sm0"])
        P.op("dve", lambda e: e.reduce_sum(out=small[:, 1:2], in_=ftmp[:, 1, 64:128], axis=AX.X), reads=[("ft", 1)], writes=["sm1"])
        act(small[:, 3:4], small[:, 0:1], AF.Exp, ["sm0"], ["sm3"])
        act(small[:, 4:5], small[:, 1:2], AF.Exp, ["sm1"], ["sm4"])
        tt("dve", small[:, 5:6], small[:, 4:5], small[:, 3:4], ALU.subtract, ["sm3", "sm4"], ["sm5"])
        tsc("dve", small[:, 2:3], small[:, 5:6], -LAM_INIT, None, ALU.add, None, ["sm5"], ["neglam"])
        tsc("dve", gc[:, 47:48], gc[:, 46:47], 1.0 - LAM_INIT, None, ALU.mult, None, ["gc"], ["gc"])

        if stage == 0:
            raise _Stop
        def cast(dst, src, key):
            P.dma(dst, src, writes=[key], queue="pool")

        def cast_cols(name, src, c0, nslab, w):
            for s in range(nslab):
                cast(slabs[name][s], src[:, c0 + s * w:c0 + (s + 1) * w].rearrange("(kc p) n -> p kc n", p=128),
                     ("slab", name, s))

        cast_cols("ckv", w_in_d, 512, 1, 256)
        cast(slabs["kpe"][0][:, :, 0:64], w_in_d[:, 768:832].rearrange("(kc p) n -> p kc n", p=128), ("slab", "kpe", 0))
        cast(slabs["kpe"][0][:, :, 64:128], w_kpeP_d.rearrange("(kc p) n -> p kc n", p=128), ("slab", "kpe", 0))
        for kc in range(2):
            src = wkv_b_d[kc * 128:(kc + 1) * 128, :].rearrange("p (h e) -> p h e", e=256)
            cast(slabs["wkvk"][0][:, kc, :].rearrange("p (h d) -> p h d", d=128), src[:, :, 0:128], ("slab", "wkvk", 0))
            cast(slabs["wkvv"][0][:, kc, :].rearrange("p (h d) -> p h d", d=128), src[:, :, 128:256], ("slab", "wkvv", 0))
        cast_cols("dk", w_in_d, 1856, 2, 512)
        cast_cols("dv", w_in_d, 2880, 2, 512)
        cast_cols("cq", w_in_d, 0, 1, 512)
        for kc in range(4):
            dst = slabs["wqb"][0][:, kc, :].rearrange("p (h d) -> p h d", d=256)
            cast(dst[:, :, 0:192], wq_b_d[kc * 128:(kc + 1) * 128, :].rearrange("p (h d) -> p h d", d=192), ("slab", "wqb", 0))
            cast(dst[:, :, 192:256], wq_ropeP_d[kc * 128:(kc + 1) * 128, :].rearrange("p (h d) -> p h d", d=64), ("slab", "wqb", 0))
        cast_cols("dq", w_in_d, 832, 2, 512)
        cast_cols("ga", w_in_d, 3904, 4, 512)
        cast_cols("gb", w_in_d, 5952, 4, 512)
        cast_cols("upa", w_up_mla_d, 0, 4, 512)
        cast_cols("upd", w_up_diff_d, 0, 4, 512)
        cast_cols("wo", w_o_d, 0, 4, 512)
        cast_cols("wg", w_gate_d, 0, 11, 512)
        cast_cols("wu", w_up_d, 0, 11, 512)
        for n in range(4):
            for part in range(4):
                cast(slabs["wd"][n * 4 + part],
                     w_down_d[part * 1408:(part + 1) * 1408, n * 512:(n + 1) * 512].rearrange("(kc p) n -> p kc n", p=128),
                     ("slab", "wd", n * 4 + part))

        if CAST_BARRIER:
            P.drain("sp", queues=("pool",))
        if stage == 1:
            raise _Stop
        ohw_sb = xbuf[0:32, 0:LW]
        fsel_sb = xbuf[0:32, 2048:2048 + NF]
        P.dma(ohw_sb, ohw_d, writes=["xb0"])
        P.dma(fsel_sb, farsel_d, writes=["xb1"])
        for h in range(8):
            lh = xbuf[0:32, 4096:4096 + 128]
            tsc("dve", lh, ones32[:], relbT[:, h:h + 1], None, ALU.mult, None, ["ones32", "relbT"], ["xb2"])
            wrep = xbuf[:, 6144:6144 + LW]
            for c0 in range(0, LW, 512):
                n = min(512, LW - c0)
                pp, pk = ps_new()
                mm(pp[:, 0:n], lh, ohw_sb[:, c0:c0 + n], True, True, ["xb2", "xb0"], [pk])
                act(wrep[:, c0:c0 + n], pp[:, 0:n], AF.Copy, [pk], ["xb3"])
            P.dma(rrep_d[h], wrep, reads=["xb3"])
            fr = xbuf[0:32, 5120:5120 + NF]
            tsc("dve", fr, fsel_sb, relbT[:, h:h + 1], None, ALU.mult, None, ["xb1", "relbT"], ["xb2b"])
            for c0 in range(0, NF, 512):
                n = min(512, NF - c0)
                pp, pk = ps_new()
                mm(pp[:, 0:n], ones32[:], fr[:, c0:c0 + n], True, True, ["xb2b", "ones32"], [pk])
                act(farb[:, c0:c0 + n], pp[:, 0:n], AF.Copy, [pk], ["farb"])
            P.dma(farbD_d[h], farb[:], reads=["farb"])
            P.op("dve", lambda e, h=h: e.tensor_copy(out=small[:, 8 + h:9 + h], in_=farb[:, NF - 2:NF - 1]), reads=["farb"], writes=[("f15", h)])
            P.op("dve", lambda e, h=h: e.tensor_copy(out=small[:, 16 + h:17 + h], in_=farb[:, NF - 1:NF]), reads=["farb"], writes=[("f31", h)])
        P.drain("sp", queues=("sp",))
        for h in range(8):
            P.dma(biasb[:, 0:NBW], bass.AP(rrep_t, h * 128 * LW + 127, [[LW - 1, 128], [1, NBW]]), writes=["biasb"])
            tt("dve", small[:, 24:25], selt[:, 1:2], small[:, 16 + h:17 + h], ALU.mult, ["selt", ("f31", h)], ["cL"])
            tt("dve", small[:, 25:26], selt[:, 3:4], small[:, 8 + h:9 + h], ALU.mult, ["selt", ("f15", h)], ["cR"])
            tsc("dve", biasb[:, NBW:NBW + 512], biasb[:, 640:1152], selt[:, 0:1], small[:, 24:25], ALU.mult, ALU.add,
                ["biasb", "selt", "cL"], ["biasb"])
            tsc("dve", biasb[:, NBW + 512:NBW + 1024], biasb[:, 0:512], selt[:, 2:3], small[:, 25:26], ALU.mult, ALU.add,
                ["biasb", "selt", "cR"], ["biasb"])
            P.dma(biasD_d[h], biasb[:], reads=["biasb"])
        P.drain("sp", queues=("sp",))

        if stage == 2:
            raise _Stop
        steps = []

        def add_step(fn, loads=None, att=False):
            steps.append((fn, loads, att))

        def slab_load(name, s, n):
            src = slabs[name][s].rearrange("p k w -> p (k w)")
            return [(lambda slot: slot[:, 0:n], src, [("slab", name, s)])]

        def hb(c):
            return hbuf[:, c * T:(c + 1) * T]

        def bg(c):
            return big[:, c * T:(c + 1) * T]

        def xb(t):
            return xbuf[:, t * 2048:(t + 1) * 2048]

        def xsb(t):
            return big[:, (32 + 4 * t) * T:(36 + 4 * t) * T]

        XSB_KEYS = [[("big", 32 + 4 * t + i) for i in range(4)] for t in range(4)]
        HB_KEYS = [("hb", c) for c in range(16)]

        def x_src(k0):
            return xp_d[k0:k0 + T, :] if k0 < S_P else xs_d[k0 - S_P:k0 - S_P + T, :]

        def load_x(k0):
            src = x_src(k0)
            for t in range(4):
                P.dma(xb(t), src[t * 128:(t + 1) * 128, :], writes=[("x", t)])

        def norm_T(goff):
            for t in range(4):
                act(xsb(t), xb(t), AF.Square, [("x", t)], XSB_KEYS[t] + [("st", t)], accum_out=stat[:, t:t + 1])
                act(stat[:, 4 + t:5 + t], stat[:, t:t + 1], AF.Sqrt, [("st", t)], [("st", 4 + t)], scale=1.0 / D_MODEL, bias=EPS)
                P.op("dve", lambda e, t=t: e.reciprocal(out=stat[:, 8 + t:9 + t], in_=stat[:, 4 + t:5 + t]),
                     reads=[("st", 4 + t)], writes=[("st", 8 + t)])
                tsc("dve", xsb(t), xb(t), stat[:, 8 + t:9 + t], None, ALU.mult, None,
                    [("x", t), ("st", 8 + t)], XSB_KEYS[t])

        def transposes(goff):
            for c in range(16):
                pt, pk = pt_new()
                for t in range(4):
                    P.op("pe", lambda e, t=t, c=c, pt=pt: e.transpose(out=pt[:, t * 128:(t + 1) * 128],
                                                                      in_=xsb(t)[:, c * 128:(c + 1) * 128],
                                                                      identity=identb[:]),
                         reads=XSB_KEYS[t] + ["identb"], writes=[pk])
                if c % 2 == 0:
                    tsc("dve", hb(c), pt, gc[:, goff + c:goff + c + 1], None, ALU.mult, None, [pk, "gc"], [HB_KEYS[c]])
                else:
                    P.op("act", lambda e, c=c, pt=pt: e.mul(out=hb(c), in_=pt, mul=gc[:, goff + c:goff + c + 1]),
                         reads=[pk, "gc"], writes=[HB_KEYS[c]])

        def gemm_fm(pp, pk, slot, sk, kcn, w, c0, m, rhs_fn, rhs_keys):
            for kc in range(kcn):
                mm(pp[0:m, :], slot[:, kc * w + c0:kc * w + c0 + m], rhs_fn(kc), kc == 0, kc == kcn - 1,
                   [sk] + rhs_keys, [pk])

        def square_bf(pp, pk, npart=128):
            b, bk = bt_new()
            act(b[0:npart], pp[0:npart], AF.Square, [pk], [bk])
            return b, bk

        groupsA = list(range(NKEY // T))

        def own_off(g):
            k0 = g * T
            if g < ngp:
                return k0
            if k0 >= S_P:
                return NPC + (k0 - S_P)
            return None

        def v_dst(Vd, h0, kt):
            ch, kbi = kt // CH, (kt % CH) // 128
            return Vd[h0:h0 + 4, ch, :, kbi, :].rearrange("h p d -> p h d")

        def phaseA(gi):
            g = groupsA[gi]
            k0 = g * T
            o0 = own_off(g)

            def a_norm(slot, sk):
                if gi == 0:
                    load_x(k0)
                    norm_T(GC_MIX)
                    if 1 < len(groupsA):
                        load_x(groupsA[1] * T)
                transposes(GC_MIX)
            add_step(a_norm)

            def a_prenorm(slot, sk):
                if gi + 1 < len(groupsA):
                    norm_T(GC_MIX)
                    if gi + 2 < len(groupsA):
                        load_x(groupsA[gi + 2] * T)

            def a_ckv(slot, sk):
                pps = []
                sqs = []
                for cc in range(2):
                    pp, pk = ps_new()
                    gemm_fm(pp, pk, slot, sk, 16, 256, cc * 128, 128, hb, HB_KEYS)
                    pps.append((pp, pk))
                    sqs.append(square_bf(pp, pk))
                p2, p2k = ps_new()
                for cc in range(2):
                    mm(p2, onesb[:], sqs[cc][0], cc == 0, cc == 1, ["onesb", sqs[cc][1]], [p2k])
                rs, rk = rstd_from(p2, p2k, KV_LORA)
                for cc in range(2):
                    stt(bg(cc), pps[cc][0], gc[:, GC_KVA + cc:GC_KVA + cc + 1], rs, ALU.mult, ALU.mult,
                        [pps[cc][1], "gc", rk], [("big", cc)])
            add_step(a_ckv, slab_load("ckv", 0, 16 * 256))

            def a_kpe(slot, sk):
                P.dma(ropec[:], ropeC_d[:, k0:k0 + T], writes=["ropec"])
                P.dma(ropes[:], ropeS_d[:, k0:k0 + T], writes=["ropes"])
                pr, prk = ps_new()
                gemm_fm(pr, prk, slot, sk, 16, 128, 0, 64, hb, HB_KEYS)
                prp, prpk = ps_new()
                gemm_fm(prp, prpk, slot, sk, 16, 128, 64, 64, hb, HB_KEYS)
                act(sqkpe[:], pr[0:64], AF.Square, [prk], ["sqkpe"])
                t1, k1 = ft_new()
                stt(t1[0:64], pr[0:64], gc[0:64, GC_MKR:GC_MKR + 1], ropec[:], ALU.mult, ALU.mult, [prk, "gc", "ropec"], [k1])
                t2, k2 = ft_new()
                stt(t2[0:64], prp[0:64], gc[0:64, GC_MKRP:GC_MKRP + 1], ropes[:], ALU.mult, ALU.mult, [prpk, "gc", "ropes"], [k2])
                tt("dve", kperot[:], t1[0:64], t2[0:64], ALU.add, [k1, k2], ["kperot"])
            add_step(a_kpe, slab_load("kpe", 0, 16 * 128))

            def a_wkvk(slot, sk):
                cells = {}
                cnt = {"pn": 0, "p2": 0, "k": 0}

                def pool(key, banks):
                    i = banks[cnt[key] % len(banks)]
                    cnt[key] += 1
                    return psa[i], ("ps", i)

                def k_new():
                    i = cnt["k"] % 4
                    cnt["k"] += 1
                    return btk[:, i, :], ("btk", i)

                def st1(h):
                    pn, pnk = pool("pn", [0, 1, 2, 3, 4])
                    gemm_fm(pn, pnk, slot, sk, 2, 1024, h * 128, 128, bg, [("big", 0), ("big", 1)])
                    cells[h] = (pn, pnk) + square_bf(pn, pnk)

                def st2(h):
                    pn, pnk, sq, sqk = cells.pop(h)
                    p2, p2k = pool("p2", [5, 6])
                    mm(p2, onesb[:], sq, True, False, ["onesb", sqk], [p2k])
                    mm(p2, onesb[0:64, :], sqkpe[:], False, True, ["onesb", "sqkpe"], [p2k])
                    rs, rk = rstd_from(p2, p2k, 192)
                    kn, knk = k_new()
                    stt(kn, pn, gc[:, GC_MKN:GC_MKN + 1], rs, ALU.mult, ALU.mult, [pnk, "gc", rk], [knk])
                    P.dma(KTn_d[h, :, k0:k0 + T], kn, reads=[knk])
                    kp, kpk = k_new()
                    tt("dve", kp[0:64], kperot[:], rs[0:64], ALU.mult, ["kperot", rk], [kpk])
                    P.dma(KTp_d[h, :, k0:k0 + T], kp[0:64], reads=[kpk])
                DIST = 3
                for h in range(8 + DIST):
                    if h < 8:
                        st1(h)
                    if h >= DIST:
                        st2(h - DIST)
            add_step(a_wkvk, slab_load("wkvk", 0, 2 * 1024))

            def a_wkvv(slot, sk):
                for t in range(4):
                    for n in range(2):
                        pp, pk = ps_new()
                        for cc in range(2):
                            mm(pp, bg(cc)[:, t * 128:(t + 1) * 128], slot[:, cc * 1024 + n * 512:cc * 1024 + (n + 1) * 512],
                               cc == 0, cc == 1, [sk, ("big", cc)], [pk])
                        b, bk = bt_new()
                        act(b, pp, AF.Copy, [pk], [bk])
                        P.dma(v_dst(VA_d, n * 4, k0 + t * 128), b.rearrange("p (h d) -> p h d", d=128), reads=[bk])
            add_step(a_wkvv, slab_load("wkvv", 0, 2 * 1024))

            def mk_dkq(name, s, gcol, dst_d, off):
                def fn(slot, sk):
                    cells = {}

                    def st1(i):
                        pp, pk = ps_new()
                        gemm_fm(pp, pk, slot, sk, 16, 512, i * 128, 128, hb, HB_KEYS)
                        cells[i] = (pp, pk) + square_bf(pp, pk)

                    def st2(i):
                        h = s * 4 + i
                        pp, pk, sq, sqk = cells.pop(i)
                        p2, p2k = ps_new()
                        mm(p2, oblk[:], sq, True, True, ["oblk", sqk], [p2k])
                        rs, rk = rstd_from(p2, p2k, 64)
                        kd, kdk = bt_new()
                        stt(kd, pp, gc[:, gcol:gcol + 1], rs, ALU.mult, ALU.mult, [pk, "gc", rk], [kdk])
                        P.dma(dst_d[h, :, off:off + T], kd, reads=[kdk])
                    for i in range(5):
                        if i < 4:
                            st1(i)
                        if i >= 1:
                            st2(i - 1)
                return fn
            add_step(a_prenorm)
            for s in range(2):
                add_step(mk_dkq("dk", s, GC_DK, KDT_d, k0), slab_load("dk", s, 16 * 512))

            def mk_dv(s):
                def fn(slot, sk):
                    for t in range(4):
                        pp, pk = ps_new()
                        for kc in range(16):
                            mm(pp, hb(kc)[:, t * 128:(t + 1) * 128], slot[:, kc * 512:(kc + 1) * 512], kc == 0, kc == 15,
                               [sk, HB_KEYS[kc]], [pk])
                        b, bk = bt_new()
                        act(b, pp, AF.Copy, [pk], [bk])
                        P.dma(v_dst(VD_d, s * 4, k0 + t * 128), b.rearrange("p (h d) -> p h d", d=128), reads=[bk])
                return fn
            for s in range(2):
                add_step(mk_dv(s), slab_load("dv", s, 16 * 512))

            if o0 is None:
                return

            def a_cq(slot, sk):
                pps, sqs = [], []
                for cc in range(4):
                    pp, pk = ps_new()
                    gemm_fm(pp, pk, slot, sk, 16, 512, cc * 128, 128, hb, HB_KEYS)
                    pps.append((pp, pk))
                    sqs.append(square_bf(pp, pk))
                p2, p2k = ps_new()
                for cc in range(4):
                    mm(p2, onesb[:], sqs[cc][0], cc == 0, cc == 3, ["onesb", sqs[cc][1]], [p2k])
                rs, rk = rstd_from(p2, p2k, Q_LORA)
                for cc in range(4):
                    stt(bg(2 + cc), pps[cc][0], gc[:, GC_QA + cc:GC_QA + cc + 1], rs, ALU.mult, ALU.mult,
                        [pps[cc][1], "gc", rk], [("big", 2 + cc)])
            add_step(a_cq, slab_load("cq", 0, 16 * 512))

            CQK = [("big", 2 + cc) for cc in range(4)]

            def a_wqb(slot, sk):
                cqn = lambda cc: bg(2 + cc)
                for h in range(8):
                    pn, pnk = ps_new()
                    gemm_fm(pn, pnk, slot, sk, 4, 2048, h * 256, 128, cqn, CQK)
                    pr, prk = ps_new()
                    gemm_fm(pr, prk, slot, sk, 4, 2048, h * 256 + 128, 64, cqn, CQK)
                    prp, prpk = ps_new()
                    gemm_fm(prp, prpk, slot, sk, 4, 2048, h * 256 + 192, 64, cqn, CQK)
                    sqn, sqnk = square_bf(pn, pnk)
                    sqr, sqrk = square_bf(pr, prk, 64)
                    p2, p2k = ps_new()
                    mm(p2, onesb[:], sqn, True, False, ["onesb", sqnk], [p2k])
                    mm(p2, onesb[0:64, :], sqr[0:64], False, True, ["onesb", sqrk], [p2k])
                    rs, rk = rstd_from(p2, p2k, 192)
                    qn, qnk = bt_new()
                    stt(qn, pn, gc[:, GC_MQN:GC_MQN + 1], rs, ALU.mult, ALU.mult, [pnk, "gc", rk], [qnk])
                    P.dma(QTn_d[h, :, o0:o0 + T], qn, reads=[qnk])
                    t1, k1 = ft_new()
                    stt(t1[0:64], pr[0:64], gc[0:64, GC_MQR:GC_MQR + 1], ropec[:], ALU.mult, ALU.mult, [prk, "gc", "ropec"], [k1])
                    t2, k2 = ft_new()
                    stt(t2[0:64], prp[0:64], gc[0:64, GC_MQRP:GC_MQRP + 1], ropes[:], ALU.mult, ALU.mult, [prpk, "gc", "ropes"], [k2])
                    t3, k3 = ft_new()
                    tt("dve", t3[0:64], t1[0:64], t2[0:64], ALU.add, [k1, k2], [k3])
                    qp, qpk = bt_new()
                    tt("dve", qp[0:64], t3[0:64], rs[0:64], ALU.mult, [k3, rk], [qpk])
                    P.dma(QTp_d[h, :, o0:o0 + T], qp[0:64], reads=[qpk])
            add_step(a_wqb, slab_load("wqb", 0, 4 * 2048))

            for s in range(2):
                add_step(mk_dkq("dq", s, GC_DQ, QDT_d, o0), slab_load("dq", s, 16 * 512))

        for gi in range(len(groupsA)):
            phaseA(gi)
        add_step(lambda slot, sk: P.drain("sp", queues=("sp",)))

        SC_MLA = 192.0 ** -0.5
        SC_DIF = 64.0 ** -0.5

        class Pipe:
            def __init__(self):
                self.q1 = None
                self.q2 = None

            def push(self, item):
                item[0]()
                if self.q1 is not None:
                    self.q1[1]()
                if self.q2 is not None:
                    self.q2[2]()
                self.q2 = self.q1
                self.q1 = item

            def flush(self):
                if self.q1 is not None:
                    self.q1[1]()
                if self.q2 is not None:
                    self.q2[2]()
                if self.q1 is not None:
                    self.q1[2]()
                self.q1 = self.q2 = None

        def own_group(seq, j):
            if seq == "p":
                o0, kt_base, nkb, fcol0 = j * T, 0, nkp, j * nkp
                ysrc = yp_d[j * T:(j + 1) * T, :]
                xk0 = j * T
                nown_kb, ng = NPC // 128, ngp
            else:
                o0, kt_base, nkb, fcol0 = NPC + j * T, S_P, nks, ngp * nkp + j * nks
                ysrc = ys_d[j * T:(j + 1) * T, :]
                xk0 = S_P + j * T
                nown_kb, ng = nks, ngs
            nch = nkb * 128 // CH

            def bias_kind(kb):
                if kb < nown_kb:
                    D = kb - 4 * j
                    if -1 <= D <= 4:
                        return ("tile", 512 - 128 * D)
                elif seq == "p":
                    if j == 0 and kb == nkb - 1:
                        return ("tile", NBW)
                    if j == ng - 1 and kb == nown_kb:
                        return ("tile", NBW + 512)
                return ("far", fcol0 + kb)

            for h in range(8):
                pipe = Pipe()
                for ci in range(nch):
                    kt0 = kt_base + ci * CH
                    chg = kt0 // CH
                    loads = [
                        (lambda slot: slot[:, 0:CH], KTn_d[h, :, kt0:kt0 + CH], []),
                        (lambda slot: slot[0:64, CH:2 * CH], KTp_d[h, :, kt0:kt0 + CH], []),
                        (lambda slot: slot[:, 2 * CH:3 * CH], VA_d[h, chg].rearrange("p k d -> p (k d)"), []),
                    ]

                    def fn(slot, sk, h=h, ci=ci, pipe=pipe):
                        b = h % 2
                        if h == 0 and ci == 0:
                            state["psl"] = [0, 1, 2, 5, 6, 7]
                            load_x(xk0)
                            P.dma(qbuf[:, 0, 0:T], QTn_d[0, :, o0:o0 + T], writes=[("q", 0)])
                            P.dma(qbuf[0:64, 0, T:2 * T], QTp_d[0, :, o0:o0 + T], writes=[("q", 0)])
                        if ci == 0 and h + 1 < 8:
                            P.dma(qbuf[:, 1 - b, 0:T], QTn_d[h + 1, :, o0:o0 + T], writes=[("q", 1 - b)])
                            P.dma(qbuf[0:64, 1 - b, T:2 * T], QTp_d[h + 1, :, o0:o0 + T], writes=[("q", 1 - b)])
                        qn = qbuf[:, b, 0:T]
                        qp = qbuf[0:64, b, T:2 * T]
                        for kbi in range(NKB_CH):
                            kb = ci * NKB_CH + kbi
                            first, last = (kb == 0), (kb == nkb - 1)
                            cell = {}

                            def s_fn(cell=cell, kbi=kbi):
                                pS, pSk = ps_new()
                                cell["pS"] = (pS, pSk)
                                mm(pS, slot[:, kbi * 128:(kbi + 1) * 128], qn, True, False, [sk, ("q", b)], [pSk])
                                mm(pS, slot[0:64, CH + kbi * 128:CH + (kbi + 1) * 128], qp, False, True, [sk, ("q", b)], [pSk])

                            def e_fn(cell=cell):
                                pS, pSk = cell["pS"]
                                pt_, ptk = bt_new()
                                cell["pt"] = (pt_, ptk)
                                act(pt_, pS, AF.Exp, [pSk], [ptk], scale=SC_MLA)

                            def p_fn(cell=cell, kbi=kbi, first=first, last=last):
                                pt_, ptk = cell["pt"]
                                mm(psa[3], slot[:, 2 * CH + kbi * 128:2 * CH + (kbi + 1) * 128], pt_, first, last,
                                   [sk, ptk], [("ps", 3)])
                                mm(psa[4], onesb[:], pt_, first, last, ["onesb", ptk], [("ps", 4)])
                            pipe.push((s_fn, e_fn, p_fn))
                        if ci == nch - 1:
                            pipe.flush()
                            rc, rck = ft_new()
                            P.op("dve", lambda e, rc=rc: e.reciprocal(out=rc, in_=psa[4]), reads=[("ps", 4)], writes=[rck])
                            tt("dve", bg(16 + h), psa[3], rc, ALU.mult, [("ps", 3), rck], [("big", 16 + h)])
                    add_step(fn, loads, att=True)

            for h in range(8):
                pipe = Pipe()
                for ci in range(nch):
                    kt0 = kt_base + ci * CH
                    chg = kt0 // CH
                    loads = [
                        (lambda slot: slot[:, 0:CH], KDT_d[h, :, kt0:kt0 + CH], []),
                        (lambda slot: slot[:, CH:2 * CH], VD_d[h, chg].rearrange("p k d -> p (k d)"), []),
                    ]

                    def fn(slot, sk, h=h, ci=ci, pipe=pipe):
                        b = h % 2
                        if ci == 0:
                            if h == 0:
                                state["psl"] = [0, 1, 2, 7]
                                P.dma(qbuf[:, 0, 0:T], QDT_d[0, :, o0:o0 + T], writes=[("q", 0)])
                            if h + 1 < 8:
                                P.dma(qbuf[:, 1 - b, 0:T], QDT_d[h + 1, :, o0:o0 + T], writes=[("q", 1 - b)])
                            P.dma(biasb[:], biasD_d[h], writes=["biasb"])
                            P.dma(farb[:], farbD_d[h], writes=["farb"])
                        qd = qbuf[:, b, 0:T]
                        for kbi in range(NKB_CH):
                            kb = ci * NKB_CH + kbi
                            first, last = (kb == 0), (kb == nkb - 1)
                            kind = bias_kind(kb)
                            cell = {}

                            def s_fn(cell=cell, kbi=kbi):
                                for m in range(2):
                                    pS, pSk = ps_new()
                                    cell["pS", m] = (pS, pSk)
                                    mm(pS, slot[m * 64:(m + 1) * 64, kbi * 128:(kbi + 1) * 128], qd[m * 64:(m + 1) * 64, :],
                                       True, True, [sk, ("q", b)], [pSk])

                            def e_fn(cell=cell, kind=kind):
                                for m in range(2):
                                    pS, pSk = cell["pS", m]
                                    pt_, ptk = bt_new()
                                    cell["pt", m] = (pt_, ptk)
                                    if kind[0] == "far":
                                        act(pt_, pS, AF.Exp, [pSk, "farb"], [ptk], scale=SC_DIF,
                                            bias=farb[:, kind[1]:kind[1] + 1])
                                    else:
                                        tmp, tk = ft_new()
                                        stt(tmp, pS, SC_DIF, biasb[:, kind[1]:kind[1] + T], ALU.mult, ALU.add,
                                            [pSk, "biasb"], [tk])
                                        act(pt_, tmp, AF.Exp, [tk], [ptk])

                            def p_fn(cell=cell, kbi=kbi, first=first, last=last):
                                for m in range(2):
                                    pt_, ptk = cell["pt", m]
                                    mm(psa[3 + 2 * m], slot[:, CH + kbi * 128:CH + (kbi + 1) * 128], pt_, first, last,
                                       [sk, ptk], [("ps", 3 + 2 * m)])
                                    mm(psa[4 + 2 * m], onesb[:], pt_, first, last, ["onesb", ptk], [("ps", 4 + 2 * m)])
                            pipe.push((s_fn, e_fn, p_fn))
                        if ci == nch - 1:
                            pipe.flush()
                            rc0, k0_ = ft_new()
                            P.op("dve", lambda e, rc0=rc0: e.reciprocal(out=rc0, in_=psa[4]), reads=[("ps", 4)], writes=[k0_])
                            t0, tk0 = ft_new()
                            tt("dve", t0, psa[3], rc0, ALU.mult, [("ps", 3), k0_], [tk0])
                            rc1, k1_ = ft_new()
                            P.op("dve", lambda e, rc1=rc1: e.reciprocal(out=rc1, in_=psa[6]), reads=[("ps", 6)], writes=[k1_])
                            tsc("dve", rc1, rc1, small[:, 2:3], None, ALU.mult, None, [k1_, "neglam"], [k1_])
                            t1, tk1 = ft_new()
                            tt("dve", t1, psa[5], rc1, ALU.mult, [("ps", 5), k1_], [tk1])
                            ob, obk = ft_new()
                            tt("dve", ob, t0, t1, ALU.add, [tk0, tk1], [obk])
                            sq, sqk = bt_new()
                            tt("dve", sq, ob, ob, ALU.mult, [obk], [sqk])
                            p2, p2k = ps_new()
                            mm(p2, onesb[:], sq, True, True, ["onesb", sqk], [p2k])
                            rs, rk = rstd_from(p2, p2k, 128)
                            stt(bg(24 + h), ob, gc[:, GC_SUB:GC_SUB + 1], rs, ALU.mult, ALU.mult, [obk, "gc", rk], [("big", 24 + h)])
                            if h == 7:
                                state["psl"] = list(range(7))
                    add_step(fn, loads, att=True)

            def c_norm(slot, sk):
                norm_T(GC_MIX)
                transposes(GC_MIX)
            add_step(c_norm)

            for s in range(4):
                cellg = {}

                def c_ga(slot, sk, s=s, cellg=cellg):
                    for c in range(4):
                        pp, pk = ps_new()
                        gemm_fm(pp, pk, slot, sk, 16, 512, c * 128, 128, hb, HB_KEYS)
                        f, fk = ft_new()
                        act(f, pp, AF.Sigmoid, [pk], [fk])
                        cellg["a", c] = (f, fk)
                add_step(c_ga, slab_load("ga", s, 16 * 512))

                def c_upa(slot, sk, s=s, cellg=cellg):
                    for c in range(4):
                        pp, pk = ps_new()
                        gemm_fm(pp, pk, slot, sk, 8, 512, c * 128, 128, lambda kc: bg(16 + kc), [("big", 16 + i) for i in range(8)])
                        f, fk = cellg["a", c]
                        tt("dve", f, pp, f, ALU.mult, [pk, fk], [fk])
                add_step(c_upa, slab_load("upa", s, 8 * 512))

                def c_gb(slot, sk, s=s, cellg=cellg):
                    for c in range(4):
                        pp, pk = ps_new()
                        gemm_fm(pp, pk, slot, sk, 16, 512, c * 128, 128, hb, HB_KEYS)
                        f, fk = ft_new()
                        act(f, pp, AF.Sigmoid, [pk], [fk])
                        cellg["b", c] = (f, fk)
                add_step(c_gb, slab_load("gb", s, 16 * 512))

                def c_upd(slot, sk, s=s, cellg=cellg):
                    for c in range(4):
                        pp, pk = ps_new()
                        gemm_fm(pp, pk, slot, sk, 8, 512, c * 128, 128, lambda kc: bg(24 + kc), [("big", 24 + i) for i in range(8)])
                        f, fk = cellg["b", c]
                        tt("dve", f, pp, f, ALU.mult, [pk, fk], [fk])
                        fa, fak = cellg["a", c]
                        tt("dve", bg(4 * s + c), fa, f, ALU.add, [fak, fk], [("big", 4 * s + c)])
                add_step(c_upd, slab_load("upd", s, 8 * 512))

            MK = [("big", i) for i in range(16)]
            for n in range(4):
                def c_wo(slot, sk, n=n):
                    for t in range(4):
                        pp, pk = ps_new()
                        for kc in range(16):
                            mm(pp, bg(kc)[:, t * 128:(t + 1) * 128], slot[:, kc * 512:(kc + 1) * 512], kc == 0, kc == 15,
                               [sk, MK[kc]], [pk])
                        xs_ = xb(t)[:, n * 512:(n + 1) * 512]
                        tt("dve", xs_, pp, xs_, ALU.add, [pk, ("x", t)], [("x", t)])
                add_step(c_wo, slab_load("wo", n, 16 * 512))

            def f_norm(slot, sk):
                norm_T(GC_FFN)
                transposes(GC_FFN)
            add_step(f_norm)

            for s in range(11):
                cellf = {}

                def f_g(slot, sk, s=s, cellf=cellf):
                    for c in range(4):
                        pp, pk = ps_new()
                        gemm_fm(pp, pk, slot, sk, 16, 512, c * 128, 128, hb, HB_KEYS)
                        f, fk = ft_new()
                        act(f, pp, AF.Silu, [pk], [fk])
                        cellf[c] = (f, fk)
                add_step(f_g, slab_load("wg", s, 16 * 512))

                def f_u(slot, sk, s=s, cellf=cellf):
                    for c in range(4):
                        pp, pk = ps_new()
                        gemm_fm(pp, pk, slot, sk, 16, 512, c * 128, 128, hb, HB_KEYS)
                        f, fk = cellf[c]
                        tt("dve", bg(4 * s + c), pp, f, ALU.mult, [pk, fk], [("big", 4 * s + c)])
                add_step(f_u, slab_load("wu", s, 16 * 512))

            for n in range(4):
                for part in range(4):
                    def f_d(slot, sk, n=n, part=part):
                        for t in range(4):
                            for kc in range(11):
                                ch = part * 11 + kc
                                mm(psa[t], bg(ch)[:, t * 128:(t + 1) * 128], slot[:, kc * 512:(kc + 1) * 512],
                                   part == 0 and kc == 0, part == 3 and kc == 10, [sk, ("big", ch)], [("ps", t)])
                        if part == 3:
                            for t in range(4):
                                xs_ = xb(t)[:, n * 512:(n + 1) * 512]
                                tt("dve", xs_, psa[t], xs_, ALU.add, [("ps", t), ("x", t)], [("x", t)])
                            if n == 3:
                                for t in range(4):
                                    P.dma(ysrc[t * 128:(t + 1) * 128, :], xb(t), reads=[("x", t)])
                    add_step(f_d, slab_load("wd", n * 4 + part, 11 * 512))

        if stage >= 4:
            for j in range(ngp):
                own_group("p", j)
        if stage >= 4:
            for j in range(ngs):
                own_group("s", j)

        load_steps = [i for i, (fn, l, a) in enumerate(steps) if l]
        issued = [0]

        def issue_upto(m):
            while issued[0] <= m and issued[0] < len(load_steps):
                idx = load_steps[issued[0]]
                si = issued[0] % 3
                for (dst_fn, src, rk) in steps[idx][1]:
                    P.dma(dst_fn(wslot[si]), src, reads=rk, writes=[("wslot", si)])
                issued[0] += 1

        nexec = 0
        for i, (fn, loads, att) in enumerate(steps):
            if max_steps is not None and i >= max_steps:
                break
            if loads:
                prev_att = nexec > 0 and steps[load_steps[nexec - 1]][2]
                issue_upto(nexec + (1 if prev_att else 2))
                si = nexec % 3
                fn(wslot[si], ("wslot", si))
                nexec += 1
            else:
                fn(None, None)

    return nc, P, st, body


_CACHE = {}


def run(inputs, S_P, S_S, debug=False, stage=9, max_steps=None, ncores=NCORES, first=0):
    key = (S_P, S_S, debug, stage, max_steps)
    if key not in _CACHE:
        _CACHE[key] = build_program(S_P, S_S, debug, stage, max_steps)
    nc = _CACHE[key]
    maps = host_prep(inputs, S_P, S_S)
    res = run_bass_kernel_spmd(nc, maps[first:first + ncores], core_ids=list(range(ncores)))
    return res


SINGLE_LAUNCH = True


def kernel(**inputs):
    S_P = int(np.asarray(inputs["x_prompt"]).shape[1])
    S_S = int(np.asarray(inputs["x_sample"]).shape[1])
    NPC = S_P // NCORES
    yp = np.empty((1, S_P, D_MODEL), np.float32)
    ys = np.empty((NCORES, S_S, D_MODEL), np.float32)
    if SINGLE_LAUNCH:
        res = run(inputs, S_P, S_S)
        for c in range(NCORES):
            yp[0, c * NPC:(c + 1) * NPC] = np.asarray(res.results[c]["yp"], dtype=np.float32)
            ys[c] = np.asarray(res.results[c]["ys"], dtype=np.float32)
        return (yp, ys)
    key = (S_P, S_S, False, 9, None)
    if key not in _CACHE:
        _CACHE[key] = build_program(S_P, S_S)
    nc = _CACHE[key]
    maps = host_prep(inputs, S_P, S_S)
    for c in range(NCORES):
        res = run_bass_kernel_spmd(nc, [maps[c]], core_ids=[0])
        yp[0, c * NPC:(c + 1) * NPC] = np.asarray(res.results[0]["yp"], dtype=np.float32)
        ys[c] = np.asarray(res.results[0]["ys"], dtype=np.float32)
        maps[c] = None
    return (yp, ys)
```
